# Optimizing a Trainium2 kernel written in Bass

```python
import jax
import jax.numpy as jnp
from jax import lax
import numpy as np

D_MODEL = 2048
BATCH = 32
SEQ = 256
DEPTH = 2
DEC_BATCH = 8
DEC_SEQ = 1024
PAST_LEN = 512

GRID_W = 64
ROPE_THETA = 10000.0
HEAD_DIM = 128
MIX_W = D_MODEL
MLA_HEADS = 8
MLA_NOPE = 128
MLA_ROPE = 64
MLA_V = 128
MLA_QK = MLA_NOPE + MLA_ROPE
Q_LORA = 512
KV_LORA = 256
NA_HEADS = 4
NA_WIN_H = 8
NA_WIN_W = 16
NA_QCOLS = 16
NA_BAND_W = 32
NA_NCB = GRID_W // NA_QCOLS
GQA_HEADS = 4
GQA_KV_HEADS = 2
GQA_GROUP = GQA_HEADS // GQA_KV_HEADS
D_FF = 5632
Q_BLOCK = 128
EPS = 1e-6
NEG_INF = -1e30
N_MOD = 9
_S1 = Q_LORA
_S2 = _S1 + KV_LORA
_S3 = _S2 + MLA_ROPE
_S4 = _S3 + 3 * NA_HEADS * HEAD_DIM
_S5 = _S4 + GQA_HEADS * HEAD_DIM
IN_COLS = _S5 + 2 * GQA_KV_HEADS * HEAD_DIM
IN_SPLITS = (_S1, _S2, _S3, _S4, _S5)

kernel_name = 'hybrid_prefix_diffusion_step'


def rms_norm(x, g):
    xf = x.astype(jnp.float32)
    y = xf * lax.rsqrt(jnp.mean(xf * xf, axis=-1, keepdims=True) + EPS)
    return (y * g.astype(jnp.float32)).astype(x.dtype)


def axial_rope(x):
    B, S, H, d = x.shape
    quarter = d // 4
    t = jnp.arange(S)
    pos = jnp.stack([t // GRID_W, t % GRID_W], axis=-1).astype(jnp.float32)
    inv = ROPE_THETA ** (-jnp.arange(quarter, dtype=jnp.float32) / quarter)
    ang = pos[:, :, None] * inv
    cos = jnp.cos(ang)[None, :, None]
    sin = jnp.sin(ang)[None, :, None]
    xr = x.astype(jnp.float32).reshape(B, S, H, 2, 2, quarter)
    x1, x2 = xr[..., 0, :], xr[..., 1, :]
    out = jnp.stack([x1 * cos - x2 * sin, x2 * cos + x1 * sin], axis=-2)
    return out.reshape(B, S, H, d).astype(x.dtype)


def attend(q, k, v):
    B, Sq, Hk, G, Dk = q.shape
    nb = Sq // Q_BLOCK
    scale = Dk ** -0.5
    qb = jnp.moveaxis(q.reshape(B, nb, Q_BLOCK, Hk, G, Dk), 1, 0)

    def one_block(qi):
        s = jnp.einsum('bqhgd,bkhd->bhgqk', qi, k).astype(jnp.float32) * scale
        p = jax.nn.softmax(s, axis=-1).astype(v.dtype)
        return jnp.einsum('bhgqk,bkhe->bqhge', p, v)

    o = lax.map(one_block, qb)
    return jnp.moveaxis(o, 0, 1).reshape(B, Sq, Hk * G, v.shape[-1])


def neighbourhood_attention(q, k, v, k_ctx, v_ctx, rpb):
    B, S, H, D = q.shape
    rows = S // GRID_W
    wh = min(NA_WIN_H, rows)
    nk = wh * NA_BAND_W
    scale = D ** -0.5
    r = jnp.arange(rows)
    row_start = jnp.clip(r - wh // 2, 0, rows - wh)
    key_rows = row_start[:, None] + jnp.arange(wh)
    q_cols = jnp.arange(GRID_W).reshape(NA_NCB, NA_QCOLS)
    band_start = jnp.clip(q_cols[:, 0] - NA_WIN_W // 2, 0, GRID_W - NA_BAND_W)
    key_cols = band_start[:, None] + jnp.arange(NA_BAND_W)
    col_start = jnp.clip(q_cols - NA_WIN_W // 2, 0, GRID_W - NA_WIN_W)
    kc = key_cols[:, None, :]
    valid = (kc >= col_start[:, :, None]) & (kc < col_start[:, :, None] + NA_WIN_W)
    dr = key_rows - r[:, None] + (NA_WIN_H - 1)
    dc = jnp.clip(kc - q_cols[:, :, None] + (NA_WIN_W - 1), 0, 2 * NA_WIN_W - 2)
    bias = rpb[:, dr[:, None, None, :, None], dc[None, :, :, None, :]]
    bias = jnp.where(valid[None, None, :, :, None, :], bias.astype(jnp.float32), NEG_INF)
    bias = bias.reshape(H, rows, NA_NCB, NA_QCOLS, nk).transpose(1, 2, 0, 3, 4)
    ridx = key_rows[:, None, :, None]
    cidx = key_cols[None, :, None, :]
    kb = k.reshape(B, rows, GRID_W, H, D)[:, ridx, cidx].reshape(B, rows, NA_NCB, nk, H, D)
    vb = v.reshape(B, rows, GRID_W, H, D)[:, ridx, cidx].reshape(B, rows, NA_NCB, nk, H, D)
    qg = q.reshape(B, rows, NA_NCB, NA_QCOLS, H, D)
    s_loc = jnp.einsum('brcqhd,brckhd->brchqk', qg, kb).astype(jnp.float32) * scale + bias
    s_ctx = jnp.einsum('brcqhd,bphd->brchqp', qg, k_ctx).astype(jnp.float32) * scale
    p = jax.nn.softmax(jnp.concatenate([s_loc, s_ctx], axis=-1), axis=-1).astype(v.dtype)
    o = (jnp.einsum('brchqk,brckhd->brcqhd', p[..., :nk], vb)
         + jnp.einsum('brchqp,bphd->brcqhd', p[..., nk:], v_ctx))
    return o.reshape(B, S, H * D)


def _mixer_inputs(h, lp):
    B, S, _ = h.shape
    c_q, c_kv, k_rope, na_qkv, g_q, g_kv = jnp.split(h @ lp['w_in'], IN_SPLITS, axis=-1)
    q_mla = (rms_norm(c_q, lp['mla_q_norm']) @ lp['mla_wqb']).reshape(B, S, MLA_HEADS, MLA_QK)
    c_kv = rms_norm(c_kv, lp['mla_kv_norm'])
    na_qkv = na_qkv.reshape(B, S, 3, NA_HEADS, HEAD_DIM)
    g_q = rms_norm(g_q.reshape(B, S, GQA_HEADS, HEAD_DIM), lp['gqa_q_norm'])
    g_kv = g_kv.reshape(B, S, 2, GQA_KV_HEADS, HEAD_DIM)
    g_k = rms_norm(g_kv[:, :, 0], lp['gqa_k_norm'])
    return (q_mla, c_kv, k_rope, na_qkv[:, :, 0], na_qkv[:, :, 1], na_qkv[:, :, 2],
            g_q, g_k, g_kv[:, :, 1])


def _mla_kv(c_kv, k_rope, w_kvb):
    B, S, _ = c_kv.shape
    kv = (c_kv @ w_kvb).reshape(B, S, MLA_HEADS, MLA_NOPE + MLA_V)
    k = jnp.concatenate([kv[..., :MLA_NOPE],
                         jnp.broadcast_to(k_rope[:, :, None, :], (B, S, MLA_HEADS, MLA_ROPE))], axis=-1)
    return k, kv[..., MLA_NOPE:]


def _context_mixer(h, lp):
    B, S, _ = h.shape
    q_mla, c_kv, k_rope, na_q, na_k, na_v, g_q, g_k, g_v = _mixer_inputs(h, lp)
    k_mla, v_mla = _mla_kv(c_kv, k_rope, lp['mla_wkvb'])
    o_a = attend(q_mla[:, :, :, None], k_mla, v_mla).reshape(B, S, -1)
    o_b = attend(na_q[:, :, :, None], na_k, na_v).reshape(B, S, -1)
    o_c = attend(g_q.reshape(B, S, GQA_KV_HEADS, GQA_GROUP, HEAD_DIM), g_k, g_v).reshape(B, S, -1)
    return jnp.concatenate([o_a, o_b, o_c], axis=-1), (c_kv, k_rope, na_k, na_v, g_k, g_v)


def _latent_mixer(h, lp, caches):
    ckv_c, krope_c, nak_c, nav_c, gk_c, gv_c = caches
    B, S, _ = h.shape
    q_mla, c_kv, k_rope, na_q, na_k, na_v, g_q, g_k, g_v = _mixer_inputs(h, lp)
    q_mla = jnp.concatenate([q_mla[..., :MLA_NOPE], axial_rope(q_mla[..., MLA_NOPE:])], axis=-1)
    k_rope = axial_rope(k_rope[:, :, None, :])[:, :, 0]
    k_mla, v_mla = _mla_kv(jnp.concatenate([ckv_c, c_kv], axis=1),
                           jnp.concatenate([krope_c, k_rope], axis=1), lp['mla_wkvb'])
    o_a = attend(q_mla[:, :, :, None], k_mla, v_mla).reshape(B, S, -1)
    o_b = neighbourhood_attention(na_q, na_k, na_v, nak_c, nav_c, lp['na_rpb'])
    g_q = axial_rope(g_q)
    g_k = axial_rope(g_k)
    o_c = attend(g_q.reshape(B, S, GQA_KV_HEADS, GQA_GROUP, HEAD_DIM),
                 jnp.concatenate([gk_c, g_k], axis=1),
                 jnp.concatenate([gv_c, g_v], axis=1)).reshape(B, S, -1)
    return jnp.concatenate([o_a, o_b, o_c], axis=-1), ()


def _swiglu(h, wg, wu, wd):
    return (jax.nn.silu(h @ wg) * (h @ wu)) @ wd


def _modulation(cond, w, b):
    m = jax.nn.silu(cond) @ w + b
    return jnp.split(m[:, None, :], N_MOD, axis=-1)


def _modulate(x, g, shift, scale):
    return rms_norm(x, g) * (1.0 + scale) + shift


def _block(x, mods, lp, mixer):
    sh1, sc1, g1, sh2, sc2, g2, sh3, sc3, g3 = mods
    h = _modulate(x, lp['norm_g'][0], sh1, sc1)
    x = x + 0.5 * g1 * _swiglu(h, lp['ffn_wg'][0], lp['ffn_wu'][0], lp['ffn_wd'][0])
    o, state = mixer(_modulate(x, lp['norm_g'][1], sh2, sc2))
    x = x + g2 * (o @ lp['w_out'])
    h = _modulate(x, lp['norm_g'][2], sh3, sc3)
    x = x + 0.5 * g3 * _swiglu(h, lp['ffn_wg'][1], lp['ffn_wu'][1], lp['ffn_wd'][1])
    return x, state


def setup_inputs(seed: int = 0) -> dict:
    key = jax.random.key(seed)
    ks = jax.random.split(key, 26)
    f32 = jnp.float32

    def nrm(k, shape, s):
        return jax.random.normal(k, shape, f32) * s

    def gain(k, shape):
        return 1.0 + 0.02 * jax.random.normal(k, shape, f32)

    return {
        'x_prompt': nrm(ks[0], (BATCH, SEQ, D_MODEL), 1.0),
        'x_sample': nrm(ks[1], (DEC_BATCH, DEC_SEQ, D_MODEL), 1.0),
        'cache_mla_ckv': nrm(ks[2], (DEC_BATCH, DEPTH, PAST_LEN, KV_LORA), 1.0),
        'cache_mla_krope': nrm(ks[3], (DEC_BATCH, DEPTH, PAST_LEN, MLA_ROPE), 1.0),
        'cache_na_k': nrm(ks[4], (DEC_BATCH, DEPTH, PAST_LEN, NA_HEADS, HEAD_DIM), 1.0),
        'cache_na_v': nrm(ks[5], (DEC_BATCH, DEPTH, PAST_LEN, NA_HEADS, HEAD_DIM), 1.0),
        'cache_gqa_k': nrm(ks[6], (DEC_BATCH, DEPTH, PAST_LEN, GQA_KV_HEADS, HEAD_DIM), 1.0),
        'cache_gqa_v': nrm(ks[7], (DEC_BATCH, DEPTH, PAST_LEN, GQA_KV_HEADS, HEAD_DIM), 1.0),
        'c': nrm(ks[8], (DEC_BATCH, D_MODEL), 1.0),
        'c_ctx': nrm(ks[9], (D_MODEL,), 1.0),
        'ada_w': nrm(ks[10], (DEPTH, D_MODEL, N_MOD * D_MODEL), 0.5 * D_MODEL ** -0.5),
        'ada_b': nrm(ks[11], (DEPTH, N_MOD * D_MODEL), 0.02),
        'norm_g': gain(ks[12], (DEPTH, 3, D_MODEL)),
        'ffn_wg': nrm(ks[13], (DEPTH, 2, D_MODEL, D_FF), D_MODEL ** -0.5),
        'ffn_wu': nrm(ks[14], (DEPTH, 2, D_MODEL, D_FF), D_MODEL ** -0.5),
        'ffn_wd': nrm(ks[15], (DEPTH, 2, D_FF, D_MODEL), D_FF ** -0.5),
        'w_in': nrm(ks[16], (DEPTH, D_MODEL, IN_COLS), D_MODEL ** -0.5),
        'mla_q_norm': gain(ks[17], (DEPTH, Q_LORA)),
        'mla_wqb': nrm(ks[18], (DEPTH, Q_LORA, MLA_HEADS * MLA_QK), Q_LORA ** -0.5),
        'mla_kv_norm': gain(ks[19], (DEPTH, KV_LORA)),
        'mla_wkvb': nrm(ks[20], (DEPTH, KV_LORA, MLA_HEADS * (MLA_NOPE + MLA_V)), KV_LORA ** -0.5),
        'na_rpb': nrm(ks[21], (DEPTH, NA_HEADS, 2 * NA_WIN_H - 1, 2 * NA_WIN_W - 1), 0.1),
        'gqa_q_norm': gain(ks[22], (DEPTH, HEAD_DIM)),
        'gqa_k_norm': gain(ks[23], (DEPTH, HEAD_DIM)),
        'w_out': nrm(ks[24], (DEPTH, MIX_W, D_MODEL), MIX_W ** -0.5),
        'final_norm': gain(ks[25], (D_MODEL,)),
    }


def reference(x_prompt, x_sample, cache_mla_ckv, cache_mla_krope, cache_na_k, cache_na_v,
              cache_gqa_k, cache_gqa_v, c, c_ctx, ada_w, ada_b, norm_g, ffn_wg, ffn_wu, ffn_wd,
              w_in, mla_q_norm, mla_wqb, mla_kv_norm, mla_wkvb, na_rpb, gqa_q_norm, gqa_k_norm,
              w_out, final_norm):
    xp = x_prompt
    xs = x_sample
    ctx_states = []
    for l in range(DEPTH):
        lp = {
            'norm_g': norm_g[l], 'ffn_wg': ffn_wg[l], 'ffn_wu': ffn_wu[l], 'ffn_wd': ffn_wd[l],
            'w_in': w_in[l], 'mla_q_norm': mla_q_norm[l], 'mla_wqb': mla_wqb[l],
            'mla_kv_norm': mla_kv_norm[l], 'mla_wkvb': mla_wkvb[l], 'na_rpb': na_rpb[l],
            'gqa_q_norm': gqa_q_norm[l], 'gqa_k_norm': gqa_k_norm[l], 'w_out': w_out[l],
        }
        mods_ctx = _modulation(c_ctx[None, :], ada_w[l], ada_b[l])
        xp, st = _block(xp, mods_ctx, lp, lambda h: _context_mixer(h, lp))
        ctx_states.append(st)
        caches = (cache_mla_ckv[:, l], cache_mla_krope[:, l], cache_na_k[:, l], cache_na_v[:, l],
                  cache_gqa_k[:, l], cache_gqa_v[:, l])
        mods_lat = _modulation(c, ada_w[l], ada_b[l])
        xs, _ = _block(xs, mods_lat, lp, lambda h: _latent_mixer(h, lp, caches))
    y_prompt = rms_norm(xp, final_norm)
    y_sample = rms_norm(xs, final_norm)
    new_mla_ckv = jnp.stack([s[0] for s in ctx_states], axis=1)
    new_mla_krope = jnp.stack([s[1] for s in ctx_states], axis=1)
    new_na_k = jnp.stack([s[2] for s in ctx_states], axis=1)
    new_na_v = jnp.stack([s[3] for s in ctx_states], axis=1)
    new_gqa_k = jnp.stack([s[4] for s in ctx_states], axis=1)
    new_gqa_v = jnp.stack([s[5] for s in ctx_states], axis=1)
    return (y_prompt, y_sample, new_mla_ckv, new_mla_krope, new_na_k, new_na_v, new_gqa_k, new_gqa_v)
```

```python
import numpy as np
from contextlib import ExitStack
import concourse.bass as bass
import concourse.mybir as mybir
from concourse.bass_utils import run_bass_kernel_spmd

F32 = mybir.dt.float32
BF16 = mybir.dt.bfloat16
AF = mybir.ActivationFunctionType
ALU = mybir.AluOpType

NCORES = 8
D = 2048
KC = 16
DFF = 5632
NFC = 44
NT = 1024
TT = 512
NTT = NT // TT
L = 2
EPS = 1e-6
PAST = 512
NS = 4
NFP = 8
NBP = 8
OVN = 24064
SLOT = 4096
S1, S2, S3 = 512, 768, 832
S4 = S3 + 1536
S5 = S4 + 512
INC = S5 + 512
NVL = 200
VB_ADAB, VB_NG, VB_QN, VB_KVN, VB_GQN, VB_GKN = 0, 144, 192, 196, 198, 199


class Op:
    __slots__ = ("eng", "fn", "deps", "sig", "rank", "dma", "dsi", "dval")


class Prog:
    def __init__(self, ndsem=8):
        self.ops = []
        self.lastw = {}
        self.rd_eng = {}
        self.rd_dma = {}
        self.ndsem = ndsem
        self.dq_cnt = {"pool": [0] * ndsem, "sp": [0] * ndsem}
        self.dq_last = {"pool": [None] * ndsem, "sp": [None] * ndsem}
        self.dq_rr = {"pool": 0, "sp": 0}

    def add(self, eng, fn, reads=(), writes=(), dma=False):
        idx = len(self.ops)
        op = Op()
        op.eng, op.fn, op.sig, op.rank, op.dma = eng, fn, False, 0, dma
        deps = set()
        for r in reads:
            w = self.lastw.get(r)
            if w is not None:
                deps.add(w)
        for r in writes:
            w = self.lastw.get(r)
            if w is not None:
                deps.add(w)
            for ri in self.rd_eng.get(r, {}).values():
                deps.add(ri)
            for ri in self.rd_dma.get(r, ()):
                deps.add(ri)
        if dma:
            k = self.dq_rr[eng] % self.ndsem
            self.dq_rr[eng] += 1
            prev = self.dq_last[eng][k]
            if prev is not None:
                deps.add(prev)
            self.dq_cnt[eng][k] += 1
            self.dq_last[eng][k] = idx
            op.dsi = k
            op.dval = 16 * self.dq_cnt[eng][k]
        for r in reads:
            if dma:
                self.rd_dma.setdefault(r, []).append(idx)
            else:
                self.rd_eng.setdefault(r, {})[eng] = idx
        for r in writes:
            self.lastw[r] = idx
            self.rd_eng[r] = {}
            self.rd_dma[r] = []
        deps.discard(idx)
        op.deps = deps
        self.ops.append(op)
        return idx

    def emit(self, nc):
        ops = self.ops
        for op in ops:
            for d in op.deps:
                dop = ops[d]
                if not dop.dma and not (dop.eng == "pe" and op.eng == "pe"):
                    dop.sig = True
        cnt = {}
        for op in ops:
            if not op.dma and op.sig:
                cnt[op.eng] = cnt.get(op.eng, 0) + 1
                op.rank = cnt[op.eng]
        with ExitStack() as es:
            esem = {e: es.enter_context(nc.semaphore("s_" + e)) for e in ("pe", "act", "dve", "pool")}
            dsem = {q: [es.enter_context(nc.semaphore("d_%s%d" % (q, i))) for i in range(self.ndsem)]
                    for q in ("pool", "sp")}
            block = es.enter_context(nc.Block())

            def make(eng_name):
                def body(e):
                    seen = {}
                    for op in ops:
                        if op.eng != eng_name:
                            continue
                        need = {}
                        for d in op.deps:
                            dop = ops[d]
                            if dop.dma:
                                key = ("d", dop.eng, dop.dsi)
                                val = dop.dval
                            else:
                                if dop.eng == "pe" and eng_name == "pe":
                                    continue
                                key = ("e", dop.eng)
                                val = dop.rank
                            if val > need.get(key, 0):
                                need[key] = val
                        for key, val in need.items():
                            if seen.get(key, 0) >= val:
                                continue
                            seen[key] = val
                            s = dsem[key[1]][key[2]] if key[0] == "d" else esem[key[1]]
                            e.wait_ge(s, val)
                        ins = op.fn(e)
                        if op.dma:
                            ins.then_inc(dsem[eng_name][op.dsi], 16)
                        elif op.sig:
                            ins.then_inc(esem[eng_name], 1)
                    if eng_name in ("pool", "sp"):
                        for k in range(self.ndsem):
                            if self.dq_cnt[eng_name][k]:
                                e.wait_ge(dsem[eng_name][k], 16 * self.dq_cnt[eng_name][k])
                return body

            block.tensor(make("pe"))
            block.scalar(make("act"))
            block.vector(make("dve"))
            block.gpsimd(make("pool"))
            block.sync(make("sp"))


class Builder:
    def __init__(self, cfg):
        self.cfg = cfg
        self.nc = bass.Bass("TRN2", target_bir_lowering=False)
        self.P = Prog()
        self.es = ExitStack()
        self.bank_i = 0
        self.ring_i = 0
        self.sq_i = 0
        self.held = set()
        self.ovl_owner = "ffn"
        self.ft_i = 0
        self.bt_i = 0

    def dram_in(self, name, shape, dt=F32):
        return self.nc.dram_tensor(name, list(shape), dt, kind="ExternalInput").ap()

    def dram_out(self, name, shape, dt=F32):
        return self.nc.dram_tensor(name, list(shape), dt, kind="ExternalOutput").ap()

    def sb(self, name, shape, dt):
        return self.es.enter_context(self.nc.sbuf_tensor(name, list(shape), dt))

    def bank(self, hold=False):
        while True:
            b = self.bank_i % 8
            self.bank_i += 1
            if b not in self.held:
                break
        if hold:
            self.held.add(b)
        return b

    def unhold(self, b):
        self.held.discard(b)

    def ft(self):
        i = self.ft_i % NFP
        self.ft_i += 1
        return ("fb", i), self.fpool[:, i, :]

    def bt(self):
        i = self.bt_i % NBP
        self.bt_i += 1
        return ("bb", i), self.bpool[:, i, :]

    def join(self, reads, writes):
        self.P.add("dve", lambda e: e.memset(self.jscr[:], 0.0), reads=list(reads) + ["jscr"], writes=list(writes) + ["jscr"])

    def wload(self, src, shape):
        lst = self.ring_ffn if self.ovl_owner == "ffn" else self.ring_mix
        s = lst[self.ring_i % len(lst)]
        self.ring_i += 1
        n = int(np.prod(shape[1:]))
        assert n <= SLOT, shape
        flat = self.ws[s][:, 0:n]
        if len(shape) == 3:
            view = flat.rearrange("p (a b) -> p a b", a=shape[1])
        else:
            view = flat
        self.P.add("pool", lambda e, o=self.split256(view), i=self.split256(src): e.dma_start(out=o, in_=i),
                   writes=[("ws", s)], dma=True)
        return s, view

    def mm(self, b, out, lhsT, rhs, start, stop, reads):
        self.P.add("pe", lambda e, o=out, l=lhsT, r=rhs, s0=start, s1=stop:
                   e.matmul(o, l, r, start=s0, stop=s1), reads=reads, writes=[("ps", b)])

    def build(self):
        nc, P, cfg = self.nc, self.P, self.cfg
        nl = cfg.get("layers", L)
        self.xT = {g: self.dram_in("xT_" + g, [128, KC, NT]) for g in "ps"}
        self.condT = self.dram_in("condT", [128, KC, 2])
        self.vecs = self.dram_in("vecs", [128, L * NVL + 16])
        self.ada_w = self.dram_in("ada_w", [L, D, 9 * D])
        self.wg = self.dram_in("ffn_wg", [L, 2, D, DFF])
        self.wu = self.dram_in("ffn_wu", [L, 2, D, DFF])
        self.wd = self.dram_in("ffn_wd", [L, 2, DFF, D])
        self.yT = {g: self.dram_out("yT_" + g, [128, KC, NT]) for g in "ps"}
        self.x = self.sb("x", [128, KC, NT], F32)
        self.h = self.sb("h", [128, KC, NT], BF16)
        self.ws = [self.sb("ws%d" % i, [128, SLOT], BF16) for i in range(NS)]
        self.ps = [self.es.enter_context(nc.psum_tensor("ps%d" % i, [128, TT], F32)) for i in range(8)]
        self.vec = self.sb("vec", [128, L * NVL + 16], F32)
        self.cond = self.sb("cond", [128, KC, 2], F32)
        self.scond = self.sb("scond", [128, KC, 2], BF16)
        self.modt = [self.sb("modt%d" % l, [128, 144, 2], F32) for l in range(L)]
        self.gs = self.sb("gs", [128, 3, KC], F32)
        self.gh = self.sb("gh", [128, 3, KC], F32)
        self.ones = self.sb("ones", [128, 128], BF16)
        self.rstd = self.sb("rstd", [128, TT], F32)
        self.fpool = self.sb("fpool", [128, NFP, TT], F32)
        self.bpool = self.sb("bpool", [128, NBP, TT], BF16)
        self.jscr = self.sb("jscr", [128, 2], F32)
        self.epsb = self.sb("epsb", [128, 2], F32)
        self.ovl = self.sb("ovl", [128, OVN], BF16)
        self.abuf = self.ovl[:, 0:12 * NT].rearrange("p (c t) -> p c t", c=12)
        self.ws = self.ws + [self.ovl[:, 12288:12288 + SLOT], self.ovl[:, 16384:16384 + SLOT]]
        self.ring_ffn = [0, 1, 2, 3, 4, 5]
        self.ring_mix = [0, 1, 2, 3]
        self.mixer_setup()
        self.nl = nl
        self.mod_bank = self.bank(hold=True)
        self.mod_next = 0
        self.mod_total = nl * 72

        P.add("pool", lambda e: e.memset(self.ones[:], 1.0), writes=["ones"])
        P.add("pool", lambda e: e.memset(self.epsb[:], EPS), writes=["epsb"])
        P.add("sp", lambda e: e.dma_start(out=self.vec[:], in_=self.vecs), writes=["vec"], dma=True)
        P.add("sp", lambda e: e.dma_start(out=self.cond[:], in_=self.condT), writes=["cond"], dma=True)
        P.add("act", lambda e: e.activation(out=self.scond[:], in_=self.cond[:], func=AF.Silu),
              reads=["cond"], writes=["scond"])
        for gi, g in enumerate("ps"):
            if g not in cfg.get("groups", "ps"):
                continue
            self.load_x(g)
            for l in range(nl):
                if cfg.get("ffn1", True):
                    self.ffn(l, 0, gi)
                if cfg.get("mixer", False):
                    self.mixer(l, gi)
                if cfg.get("ffn2", True):
                    self.ffn(l, 1, gi)
            self.final_norm(g)
        if cfg.get("dbg"):
            for name, (buf, shape, dt, res) in self.dbg_bufs().items():
                o = self.dram_out("dbg_" + name, shape, dt)
                P.add("sp", lambda e, o=o, buf=buf: e.dma_start(out=o, in_=buf), reads=res, dma=True)
        P.emit(nc)
        return nc

    def dbg_bufs(self):
        return {
            "x": (self.x[:], [128, KC, NT], F32, [("x", kc, t) for kc in range(KC) for t in range(NTT)]),
            "h": (self.h[:], [128, KC, NT], BF16, [("h", kc, t) for kc in range(KC) for t in range(NTT)]),
            "ovl": (self.ovl[:], [128, OVN], BF16, self.mix_res() + ["ez"]),
        }

    def ffn_res(self):
        return [("a", j, t) for j in range(12) for t in range(NTT)] + [("ws", 4), ("ws", 5)]

    def act(self, out, in_, func, reads, writes, **kw):
        self.P.add("act", lambda e, o=out, i=in_, f=func, kw=kw: e.activation(out=o, in_=i, func=f, **kw),
                   reads=reads, writes=writes)

    def tt(self, out, in0, in1, op, reads, writes, eng="dve"):
        self.P.add(eng, lambda e, o=out, a=in0, b=in1, op=op: e.tensor_tensor(out=o, in0=a, in1=b, op=op),
                   reads=reads, writes=writes)

    def ts(self, out, in0, s1, s2, op0, op1, reads, writes, eng="dve"):
        if op1 is None:
            self.P.add(eng, lambda e, o=out, a=in0, s1=s1, op0=op0: e.tensor_scalar(
                out=o, in0=a, scalar1=s1, scalar2=None, op0=op0), reads=reads, writes=writes)
        else:
            self.P.add(eng, lambda e, o=out, a=in0, s1=s1, s2=s2, op0=op0, op1=op1: e.tensor_scalar(
                out=o, in0=a, scalar1=s1, scalar2=s2, op0=op0, op1=op1), reads=reads, writes=writes)

    def stt(self, out, in0, scalar, in1, op0, op1, reads, writes, eng="dve"):
        self.P.add(eng, lambda e, o=out, a=in0, s=scalar, b=in1, op0=op0, op1=op1: e.scalar_tensor_tensor(
            out=o, in0=a, scalar=s, in1=b, op0=op0, op1=op1), reads=reads, writes=writes)

    def recip(self, out, in_, reads, writes):
        self.P.add("dve", lambda e, o=out, i=in_: e.reciprocal(out=o, in_=i), reads=reads, writes=writes)

    def dma(self, q, out, in_, reads=(), writes=()):
        if q == "pool":
            out, in_ = self.split256(out), self.split256(in_)
        self.P.add(q, lambda e, o=out, i=in_: e.dma_start(out=o, in_=i), reads=reads, writes=writes, dma=True)

    @staticmethod
    def split256(ap):
        n = ap.shape[-1]
        if n <= 256 or n % 256:
            return ap
        nd = len(ap.shape)
        if nd == 2:
            return ap.rearrange("p (a b) -> p a b", b=256)
        if nd == 3:
            return ap.rearrange("p c (a b) -> p c a b", b=256)
        return ap

    def mod_job(self):
        j = self.mod_next
        self.mod_next += 1
        l, ti = divmod(j, 72)
        i = ti // 8
        b = self.mod_bank
        av = self.ada_w[l].rearrange("(kc p) m -> p kc m", p=128)
        s, wv = self.wload(av[:, :, ti * 256:(ti + 1) * 256], [128, KC, 256])
        for m in range(2):
            ocl = (i % 2) * 16 + (ti % 8) * 2 + m
            for kc in range(KC):
                self.mm(b, self.ps[b][:, ocl * 2:ocl * 2 + 2], wv[:, kc, m * 128:(m + 1) * 128], self.scond[:, kc, :],
                        kc == 0, kc == KC - 1, [("ws", s), "scond"])
        if ti % 8 == 7:
            base = l * NVL + VB_ADAB + i * 16
            c0 = (i % 2) * 32
            mps = self.ps[b][:, c0:c0 + 32].rearrange("p (c s) -> p c s", s=2)
            for s_ in range(2):
                self.tt(self.modt[l][:, i * 16:(i + 1) * 16, s_], mps[:, :, s_], self.vec[:, base:base + 16], ALU.add,
                        [("ps", b), "vec"], [("modt", l, i)])
        if self.mod_next == self.mod_total:
            self.unhold(self.mod_bank)

    def mod_need(self, l, i3):
        target = min(l * 72 + (3 * i3 + 3) * 8, self.mod_total)
        while self.mod_next < target:
            self.mod_job()

    def bg(self, n=1):
        for _ in range(n):
            if self.mod_next < self.mod_total:
                self.mod_job()

    def load_x(self, g):
        for kc4 in range(0, KC, 4):
            self.dma("sp", self.x[:, kc4:kc4 + 4, :], self.xT[g][:, kc4:kc4 + 4, :],
                     writes=[("x", kc, t) for kc in range(kc4, kc4 + 4) for t in range(NTT)])

    def prep_mods(self, l, gi, i):
        target = min(l * 72 + (3 * i + 2) * 8, self.mod_total)
        while self.mod_next < target:
            self.mod_job()
        m = self.modt[l]
        sc = m[:, (3 * i + 1) * 16:(3 * i + 2) * 16, gi]
        ng = self.vec[:, l * NVL + VB_NG + i * 16: l * NVL + VB_NG + (i + 1) * 16]
        self.stt(self.gs[:, i, :], sc, 1.0, ng, ALU.add, ALU.mult, [("modt", l, 3 * i + 1), "vec"], [("gs", i)])

    def prep_gate(self, l, gi, i):
        self.mod_need(l, i)
        m = self.modt[l]
        gt = m[:, (3 * i + 2) * 16:(3 * i + 3) * 16, gi]
        self.ts(self.gh[:, i, :], gt, (1.0 if i == 1 else 0.5), None, ALU.mult, None, [("modt", l, 3 * i + 2)], [("gh", i)])

    def rstd_bcast(self, srcs, nfeat, rstd, rres):
        b = self.bank()
        n = srcs[0][0].shape[-1]
        for k, (ap, rd) in enumerate(srcs):
            qr, qa = self.bt()
            self.act(qa[:, 0:n], ap, AF.Square, rd, [qr])
            self.mm(b, self.ps[b][:, 0:n], self.ones[:], qa[:, 0:n], k == 0, k == len(srcs) - 1,
                    [qr, "ones"])
        self.act(rstd, self.ps[b][:, 0:n], AF.Ln, [("ps", b), "epsb"], [rres], scale=1.0 / nfeat, bias=self.epsb[:, 0:1])
        self.act(rstd, rstd, AF.Exp, [rres], [rres], scale=-0.5)

    def rstd_from_x(self, t):
        tsl = slice(t * TT, (t + 1) * TT)
        self.rstd_bcast([(self.x[:, kc, tsl], [("x", kc, t)]) for kc in range(KC)], D, self.rstd[:], "rstd")

    def modnorm(self, l, i, gi):
        m = self.modt[l]
        for t in range(NTT):
            tsl = slice(t * TT, (t + 1) * TT)
            self.rstd_from_x(t)
            for kc in range(KC):
                nr, na = self.ft()
                self.tt(na, self.x[:, kc, tsl], self.rstd[:], ALU.mult,
                        [("x", kc, t), "rstd"], [nr])
                sh = m[:, 3 * i * 16 + kc, gi:gi + 1]
                self.act(self.h[:, kc, tsl], na, AF.Identity,
                         [nr, ("gs", i), ("modt", l, 3 * i)], [("h", kc, t)],
                         scale=self.gs[:, i, kc:kc + 1], bias=sh)

    def ffn(self, l, fi, gi):
        i = 0 if fi == 0 else 2
        if self.ovl_owner != "ffn":
            self.join(self.mix_res() + ["ez"], self.ffn_res())
            self.ovl_owner = "ffn"
            self.ring_i = 0
        self.prep_mods(l, gi, i)
        self.modnorm(l, i, gi)
        wgv = self.wg[l, fi].rearrange("(kc p) m -> p kc m", p=128)
        wuv = self.wu[l, fi].rearrange("(kc p) m -> p kc m", p=128)
        wdv = self.wd[l, fi].rearrange("(c p) m -> p c m", p=128)
        segs = [(0, 12), (12, 12), (24, 10), (34, 10)]
        for (c0, n) in segs:
            for jt in range(n // 2):
                col = (c0 + 2 * jt) * 128
                sg_, gv = self.wload(wgv[:, :, col:col + 256], [128, KC, 256])
                su_, uv = self.wload(wuv[:, :, col:col + 256], [128, KC, 256])
                for m in range(2):
                    jl = 2 * jt + m
                    for t in range(NTT):
                        tsl = slice(t * TT, (t + 1) * TT)
                        bg = self.bank()
                        bu = self.bank()
                        for kc in range(KC):
                            self.mm(bg, self.ps[bg][:], gv[:, kc, m * 128:(m + 1) * 128], self.h[:, kc, tsl],
                                    kc == 0, kc == KC - 1, [("ws", sg_), ("h", kc, t)])
                        for kc in range(KC):
                            self.mm(bu, self.ps[bu][:], uv[:, kc, m * 128:(m + 1) * 128], self.h[:, kc, tsl],
                                    kc == 0, kc == KC - 1, [("ws", su_), ("h", kc, t)])
                        sr, sa = self.ft()
                        self.act(sa, self.ps[bg][:], AF.Silu, [("ps", bg)], [sr])
                        self.tt(self.abuf[:, jl, tsl], self.ps[bu][:], sa, ALU.mult,
                                [("ps", bu), sr], [("a", jl, t)])
                self.bg(1)
            if c0 == 0:
                self.prep_gate(l, gi, i)
            for ot in range(8):
                sd_, dv = self.wload(wdv[:, c0:c0 + n, ot * 256:(ot + 1) * 256], [128, n, 256])
                for m in range(2):
                    oc = 2 * ot + m
                    for t in range(NTT):
                        tsl = slice(t * TT, (t + 1) * TT)
                        b = self.bank()
                        for c in range(n):
                            self.mm(b, self.ps[b][:], dv[:, c, m * 128:(m + 1) * 128], self.abuf[:, c, tsl],
                                    c == 0, c == n - 1, [("ws", sd_), ("a", c, t)])
                        self.stt(self.x[:, oc, tsl], self.ps[b][:], self.gh[:, i, oc:oc + 1], self.x[:, oc, tsl],
                                 ALU.mult, ALU.add, [("ps", b), ("gh", i), ("x", oc, t)], [("x", oc, t)])
                self.bg(1)

    def final_norm(self, g):
        fb = L * NVL
        for t in range(NTT):
            tsl = slice(t * TT, (t + 1) * TT)
            self.rstd_from_x(t)
            for kc in range(KC):
                orr, oa = self.ft()
                self.stt(oa, self.x[:, kc, tsl], self.vec[:, fb + kc:fb + kc + 1], self.rstd[:],
                         ALU.mult, ALU.mult, [("x", kc, t), "rstd", "vec"], [orr])
                self.dma("sp", self.yT[g][:, kc, tsl], oa, reads=[orr])

    def mixer_setup(self):
        nc = self.nc
        o = self.ovl
        def v3(off, a, b):
            return o[:, off:off + a * b].rearrange("p (a b) -> p a b", a=a)
        self.q2 = v3(0, 2, NT)
        self.k2 = v3(2048, 2, 1536)
        self.v2 = v3(5120, 12, 256)
        self.O2 = v3(8192, 2, NT)
        self.Ob = [self.O2, v3(20992, 2, NT)]
        assert 20992 + 2 * NT <= OVN
        self.qr2 = v3(10240, 2, NT)
        self.cqn = v3(12288, 4, NT)
        self.ez = o[:, 12288:12288 + 4096].rearrange("p (h u q) -> p h u q", h=4, u=16)
        self.ckv = v3(16384, 2, 1536)
        self.kr = o[:, 19456:19456 + 1536]
        assert 19456 + 1536 <= OVN
        self.w_in = self.dram_in("w_in", [L, D, INC])
        self.wqb = self.dram_in("mla_wqb", [L, 512, 1536])
        self.wkvb = self.dram_in("mla_wkvb", [L, 256, 2048])
        self.w_out = self.dram_in("w_out", [L, D, D])
        self.c_ckvT = self.dram_in("c_ckvT", [L, 128, 2, PAST])
        self.c_krT = self.dram_in("c_krT", [L, 64, PAST])
        self.c_nakT = self.dram_in("c_nakT", [L, 4, 128, PAST])
        self.c_nav = self.dram_in("c_nav", [L, PAST, 512])
        self.c_gkT = self.dram_in("c_gkT", [L, 2, 128, PAST])
        self.c_gv = self.dram_in("c_gv", [L, PAST, 256])
        self.d_cos128 = self.dram_in("cos128", [128, NT])
        self.d_sin128 = self.dram_in("sin128", [128, NT])
        self.d_cos64 = self.dram_in("cos64", [64, NT])
        self.d_sin64 = self.dram_in("sin64", [64, NT])
        self.d_rot = self.dram_in("rotT", [128, 192])
        self.d_w1 = self.dram_in("w1", [31, 127])
        self.d_vc = self.dram_in("validc", [64, 64])
        self.d_rpbT = self.dram_in("rpbT", [31, L, 60])
        self.o_ckvT = self.dram_out("o_ckvT", [L, 128, 2, NT])
        self.o_krT = self.dram_out("o_krT", [L, 64, NT])
        self.o_nakT = self.dram_out("o_nakT", [L, 4, 128, NT])
        self.o_nav = self.dram_out("o_nav", [L, NT, 512])
        self.o_gkT = self.dram_out("o_gkT", [L, 2, 128, NT])
        self.o_gv = self.dram_out("o_gv", [L, NT, 256])
        self.rot = self.sb("rot", [128, 192], F32)
        self.w1 = self.sb("w1s", [31, 127], F32)
        self.vc = self.sb("vcs", [64, 64], BF16)
        self.vcf = self.sb("vcf", [64, 64], F32)
        self.rpb = self.sb("rpbs", [31, L, 60], F32)
        self.dma("sp", self.rot[:], self.d_rot, writes=["rot"])
        self.dma("sp", self.w1[:], self.d_w1, writes=["nac"])
        self.dma("sp", self.vcf[:], self.d_vc, writes=["vcf"])
        self.dma("sp", self.rpb[:], self.d_rpbT, writes=["nac2"])
        self.P.add("dve", lambda e: e.tensor_copy(self.vc[:], self.vcf[:]), reads=["vcf"], writes=["vc"])

    def mix_res(self):
        r = []
        for i in range(2):
            for t in range(NTT):
                r += [("q2", i, t), ("qr2", i, t), ("O2", 0, i, t), ("O2", 1, i, t)]
            for kt in range(3):
                r += [("k2", i, kt), ("ckv", i, kt)]
        r += [("kr", kt) for kt in range(3)]
        r += [("v2", c) for c in range(12)]
        r += [("cqn", oc, t) for oc in range(4) for t in range(NTT)]
        return r

    def copy(self, out, in_, reads, writes, eng="act"):
        if eng == "act":
            self.P.add("act", lambda e, o=out, i=in_: e.copy(out=o, in_=i), reads=reads, writes=writes)
        else:
            self.P.add(eng, lambda e, o=out, i=in_: e.tensor_copy(o, i), reads=reads, writes=writes)

    def rope(self, a_ap, a_res, npart, t, out_ap, out_res):
        tsl = slice(t * TT, (t + 1) * TT)
        dc, ds = (self.d_cos128, self.d_sin128) if npart == 128 else (self.d_cos64, self.d_sin64)
        rt = self.rot[:, 0:128] if npart == 128 else self.rot[0:64, 128:192]
        cr, ca = self.ft()
        self.dma("sp", ca[0:npart, :], dc[:, tsl], writes=[cr])
        sr, sa = self.ft()
        self.dma("sp", sa[0:npart, :], ds[:, tsl], writes=[sr])
        b = self.bank()
        self.mm(b, self.ps[b][0:npart, :], rt, a_ap, True, True, [a_res, "rot"])
        t1r, t1 = self.ft()
        self.tt(t1[0:npart, :], a_ap, ca[0:npart, :], ALU.mult, [a_res, cr], [t1r])
        t2r, t2 = self.ft()
        self.tt(t2[0:npart, :], self.ps[b][0:npart, :], sa[0:npart, :], ALU.mult, [("ps", b), sr], [t2r])
        self.tt(out_ap, t1[0:npart, :], t2[0:npart, :], ALU.add, [t1r, t2r], out_res)

    def build_ez(self, l):
        for q8 in range(8):
            b = self.bank()
            for qcl in range(8):
                qc = q8 * 8 + qcl
                self.mm(b, self.ps[b][0:64, qcl * 60:(qcl + 1) * 60], self.w1[:, 63 - qc:127 - qc], self.rpb[:, l, :],
                        True, True, ["nac", "nac2"])
            pin = self.ps[b][0:64, 0:480].rearrange("p (q h u) -> p h u q", q=8, h=4)
            self.act(self.ez[0:64, :, 0:15, q8 * 8:(q8 + 1) * 8], pin, AF.Exp, [("ps", b)], ["ez"])
        for hh in range(4):
            for u in range(15):
                self.tt(self.ez[0:64, hh, u, :], self.ez[0:64, hh, u, :], self.vc[:], ALU.mult, ["ez", "vc"], ["ez"])
        self.dma("sp", self.ez[64:128, :, 1:16, :], self.ez[0:64, :, 0:15, :], reads=["ez"], writes=["ez"])

    def attention(self, gi, ob, heads, scale, mla, la=2):
        items = []
        for (i, ki, vc, na_head) in heads:
            base = dict(i=i, ki=ki, vc=vc, na_head=na_head)
            if gi == 0:
                for s_ in range(4):
                    items.append(dict(base, q0=256 * s_, qn=256, chunks=[(2 * s_, None), (2 * s_ + 1, None)], first=True, last=True))
            else:
                for qt in range(2):
                    if na_head is None:
                        ch = [(c, None) for c in range(12)]
                    else:
                        r0 = 8 * qt
                        ch = [(c, None) for c in range(4)]
                        for kr0 in (range(0, 12, 2) if qt == 0 else range(4, 16, 2)):
                            halves = []
                            for hf in range(2):
                                krow = kr0 + hf
                                rs = [r for r in range(r0, r0 + 8) if min(max(r - 4, 0), 8) <= krow <= min(max(r - 4, 0), 8) + 7]
                                halves.append((rs[0], rs[-1] + 1) if rs else None)
                            ch.append((4 + kr0 // 2, (kr0, r0, halves)))
                    for idx, cn in enumerate(ch):
                        items.append(dict(base, q0=512 * qt, qn=512, chunks=[cn], first=idx == 0, last=idx == len(ch) - 1))
        qseq = -1
        for it in items:
            if it["first"]:
                qseq += 1
            it["qseq"] = qseq
            it["gi"] = gi
        self.att_pairs = [(self.bank(hold=True), self.bank(hold=True)) for _ in range(2)]
        pend = []
        for it in items:
            self.att_S(it, scale, mla)
            pend.append(it)
            if len(pend) > la:
                self.att_PV(pend.pop(0), ob)
        for it in pend:
            self.att_PV(it, ob)
        for (b0, b1) in self.att_pairs:
            self.unhold(b0)
            self.unhold(b1)

    def att_S(self, it, scale, mla):
        i, ki, q0, qn = it["i"], it["ki"], it["q0"], it["qn"]
        t = q0 // TT
        qsl = slice(q0, q0 + qn)
        bS = self.bank()
        nch = len(it["chunks"])
        for j, (c, na) in enumerate(it["chunks"]):
            csl = slice(c * 128, (c + 1) * 128)
            kt = c // 4
            osl = slice(j * qn, (j + 1) * qn)
            self.mm(bS, self.ps[bS][:, osl], self.k2[:, ki, csl], self.q2[:, i, qsl], True, not mla,
                    [("k2", ki, kt), ("q2", i, t)])
            if mla:
                self.mm(bS, self.ps[bS][:, osl], self.kr[:, csl], self.qr2[:, i, qsl], False, True,
                        [("kr", kt), ("qr2", i, t)])
        ptr, pt = self.bt()
        (c, na) = it["chunks"][0]
        if na is None:
            self.act(pt[:, 0:nch * qn], self.ps[bS][:, 0:nch * qn], AF.Exp, [("ps", bS)], [ptr], scale=scale)
        else:
            (kr0, r0, halves) = na
            self.P.add("pool", lambda e, o=pt: e.memset(o, 0.0), writes=[ptr])
            tr, tm = self.bt()
            for hf in range(2):
                if halves[hf] is None:
                    continue
                ra, rb = halves[hf]
                psl = slice(64 * hf, 64 * hf + 64)
                fsl = slice((ra - r0) * 64, (rb - r0) * 64)
                self.act(tm[psl, fsl], self.ps[bS][psl, fsl], AF.Exp, [("ps", bS)], [tr], scale=scale)
                u0 = 7 - kr0
                ezv = self.ez[psl, it["na_head"], u0 + ra:u0 + rb, :]
                self.tt(pt[psl, fsl].rearrange("p (r q) -> p r q", q=64), tm[psl, fsl].rearrange("p (r q) -> p r q", q=64),
                        ezv, ALU.mult, [tr, "ez", ptr], [ptr])
        it["pt"] = (ptr, pt)

    def att_PV(self, it, ob):
        i, vc, q0, qn = it["i"], it["vc"], it["q0"], it["qn"]
        t = q0 // TT
        qsl = slice(q0, q0 + qn)
        bO, bD = self.att_pairs[it["qseq"] % 2]
        ptr, pt = it["pt"]
        nch = len(it["chunks"])
        for j, (c, na) in enumerate(it["chunks"]):
            psl = slice(j * qn, (j + 1) * qn)
            first = it["first"] and j == 0
            last = it["last"] and j == nch - 1
            self.mm(bO, self.ps[bO][:, 0:qn], self.v2[:, c, vc * 128:(vc + 1) * 128], pt[:, psl], first, last,
                    [("v2", c), ptr])
            self.mm(bD, self.ps[bD][:, 0:qn], self.ones[:], pt[:, psl], first, last, [ptr, "ones"])
        if it["last"]:
            rr, ra_ = self.ft()
            if it["gi"] == 0:
                self.act(ra_[:, 0:qn], self.ps[bD][:, 0:qn], AF.Ln, [("ps", bD)], [rr])
                self.act(ra_[:, 0:qn], ra_[:, 0:qn], AF.Exp, [rr], [rr], scale=-1.0)
            else:
                self.recip(ra_[:, 0:qn], self.ps[bD][:, 0:qn], [("ps", bD)], [rr])
            self.tt(self.Ob[ob][:, i, qsl], self.ps[bO][:, 0:qn], ra_[:, 0:qn], ALU.mult, [("ps", bO), rr], [("O2", ob, i, t)])

    def wout_group(self, l, row0):
        wov = self.w_out[l].rearrange("(c p) m -> p c m", p=128)
        tiles = [self.wload(wov[:, row0 // 128:row0 // 128 + 4, hf * 1024:(hf + 1) * 1024], [128, 4, 1024]) for hf in range(2)]
        for oc in range(KC):
            sw, wv = tiles[oc // 8]
            col = (oc % 8) * 128
            for t in range(NTT):
                tsl = slice(t * TT, (t + 1) * TT)
                b = self.bank()
                for c in range(4):
                    self.mm(b, self.ps[b][:], wv[:, c, col:col + 128], self.Ob[c // 2][:, c % 2, tsl], c == 0, c == 3,
                            [("ws", sw), ("O2", c // 2, c % 2, t)])
                self.stt(self.x[:, oc, tsl], self.ps[b][:], self.gh[:, 1, oc:oc + 1], self.x[:, oc, tsl],
                         ALU.mult, ALU.add, [("ps", b), ("gh", 1), ("x", oc, t)], [("x", oc, t)])

    def projA(self, sw, wv, col0, m, rhs_fn, nk, t_list, n=TT):
        out = []
        for t in t_list:
            b = self.bank()
            for kc in range(nk):
                rap, rres = rhs_fn(kc, t)
                self.mm(b, self.ps[b][0:m, 0:n], wv[:, kc, col0:col0 + m], rap, kc == 0, kc == nk - 1, [("ws", sw), rres])
            out.append((t, b))
        return out

    def hrhs(self, kc, t):
        return self.h[:, kc, t * TT:(t + 1) * TT], ("h", kc, t)

    def vproj(self, l, gi, sw, wv, c0key, out_dram, ocol0):
        for c in range(8):
            t = c // 4
            b = self.bank()
            for kc in range(KC):
                self.mm(b, self.ps[b][:, 0:256], self.h[:, kc, c * 128:(c + 1) * 128], wv[:, kc, 0:256], kc == 0, kc == KC - 1,
                        [("ws", sw), ("h", kc, t)])
            if gi == 0:
                fr, fa = self.ft()
                self.copy(fa[:, 0:256], self.ps[b][:, 0:256], [("ps", b)], [fr], eng="dve")
                self.copy(self.v2[:, c0key + c, :], fa[:, 0:256], [fr], [("v2", c0key + c)])
                self.dma("sp", out_dram[l][c * 128:(c + 1) * 128, ocol0:ocol0 + 256], fa[:, 0:256], reads=[fr])
            else:
                self.copy(self.v2[:, c0key + c, :], self.ps[b][:, 0:256], [("ps", b)], [("v2", c0key + c)])

    def mixer(self, l, gi):
        self.join(self.ffn_res() + ["ez"], self.mix_res())
        self.ovl_owner = "mix"
        self.ring_i = 0
        self.prep_mods(l, gi, 1)
        self.P.add("pool", lambda e: e.memset(self.kr[64:128, :], 0.0), writes=[("kr", kt) for kt in range(3)])
        self.P.add("pool", lambda e: e.memset(self.qr2[64:128, :, :], 0.0),
                   writes=[("qr2", i, t) for i in range(2) for t in range(NTT)])
        self.modnorm(l, 1, gi)
        vb = l * NVL
        nkt = 2 if gi == 0 else 3
        koff = 0 if gi == 0 else PAST
        kc0 = koff // 128
        winv = self.w_in[l].rearrange("(kc p) m -> p kc m", p=128)
        if gi == 1:
            self.dma("pool", self.ckv[:, :, 0:PAST], self.c_ckvT[l], writes=[("ckv", 0, 0), ("ckv", 1, 0)])
            self.dma("pool", self.kr[0:64, 0:PAST], self.c_krT[l], writes=[("kr", 0)])
        wq = [self.wload(winv[:, :, j * 256:(j + 1) * 256], [128, KC, 256]) for j in range(2)]
        for t in range(NTT):
            tsl = slice(t * TT, (t + 1) * TT)
            raws = []
            for oc in range(4):
                sw, wv = wq[oc // 2]
                (_, b), = self.projA(sw, wv, (oc % 2) * 128, 128, self.hrhs, KC, [t])
                fr, fa = self.ft()
                self.copy(fa, self.ps[b][:], [("ps", b)], [fr], eng="dve")
                raws.append((fr, fa))
            self.rstd_bcast([(fa, [fr]) for fr, fa in raws], 512, self.rstd[:], "rstd")
            for oc, (fr, fa) in enumerate(raws):
                self.stt(self.cqn[:, oc, tsl], fa, self.vec[:, vb + VB_QN + oc:vb + VB_QN + oc + 1], self.rstd[:],
                         ALU.mult, ALU.mult, [fr, "rstd", "vec"], [("cqn", oc, t)])
        skv, wkv = self.wload(winv[:, :, S1:S2], [128, KC, 256])
        skr, wkr = self.wload(winv[:, :, S2:S3], [128, KC, 64])
        for t in range(NTT):
            tsl = slice(t * TT, (t + 1) * TT)
            ksl = slice(koff + t * TT, koff + (t + 1) * TT)
            kt = (koff + t * TT) // TT
            raws = []
            for oc in range(2):
                (_, b), = self.projA(skv, wkv, oc * 128, 128, self.hrhs, KC, [t])
                fr, fa = self.ft()
                self.copy(fa, self.ps[b][:], [("ps", b)], [fr], eng="dve")
                raws.append((fr, fa))
            self.rstd_bcast([(fa, [fr]) for fr, fa in raws], 256, self.rstd[:], "rstd")
            for oc, (fr, fa) in enumerate(raws):
                self.stt(fa, fa, self.vec[:, vb + VB_KVN + oc:vb + VB_KVN + oc + 1], self.rstd[:],
                         ALU.mult, ALU.mult, [fr, "rstd", "vec"], [fr])
                self.copy(self.ckv[:, oc, ksl], fa, [fr], [("ckv", oc, kt)])
                if gi == 0:
                    self.dma("sp", self.o_ckvT[l][:, oc, tsl], fa, reads=[fr])
            (_, b), = self.projA(skr, wkr, 0, 64, self.hrhs, KC, [t])
            fr, fa = self.ft()
            self.copy(fa[0:64, :], self.ps[b][0:64, :], [("ps", b)], [fr], eng="dve")
            if gi == 0:
                self.copy(self.kr[0:64, ksl], fa[0:64, :], [fr], [("kr", kt)])
                self.dma("sp", self.o_krT[l][:, tsl], fa[0:64, :], reads=[fr])
            else:
                self.rope(fa[0:64, :], fr, 64, t, self.kr[0:64, ksl], [("kr", kt)])
        if self.cfg.get("mix_stop", 9) <= 1:
            return
        self.prep_gate(l, gi, 1)
        wqbv = self.wqb[l].rearrange("(kc p) m -> p kc m", p=128)
        wkvbv = self.wkvb[l].rearrange("(kc p) m -> p kc m", p=128)
        sc_mla = 192.0 ** -0.5
        sc = 128.0 ** -0.5
        cq_rhs = lambda kc, t: (self.cqn[:, kc, t * TT:(t + 1) * TT], ("cqn", kc, t))
        ckv_rhs = lambda kc, kt: (self.ckv[:, kc, kt * TT:(kt + 1) * TT], ("ckv", kc, kt))
        for pr in range(4):
            sq_, wqp = self.wload(wqbv[:, :, pr * 384:(pr + 1) * 384], [128, 4, 384])
            sk_, wkp = self.wload(wkvbv[:, :, pr * 512:(pr + 1) * 512], [128, 2, 512])
            for i in range(2):
                for (t, b) in self.projA(sq_, wqp, i * 192, 128, cq_rhs, 4, range(NTT)):
                    self.copy(self.q2[:, i, t * TT:(t + 1) * TT], self.ps[b][:], [("ps", b)], [("q2", i, t)])
                for (t, b) in self.projA(sq_, wqp, i * 192 + 128, 64, cq_rhs, 4, range(NTT)):
                    tsl = slice(t * TT, (t + 1) * TT)
                    if gi == 0:
                        self.copy(self.qr2[0:64, i, tsl], self.ps[b][0:64, :], [("ps", b)], [("qr2", i, t)])
                    else:
                        fr, fa = self.ft()
                        self.copy(fa[0:64, :], self.ps[b][0:64, :], [("ps", b)], [fr], eng="dve")
                        self.rope(fa[0:64, :], fr, 64, t, self.qr2[0:64, i, tsl], [("qr2", i, t)])
                for (kt, b) in self.projA(sk_, wkp, i * 256, 128, ckv_rhs, 2, range(nkt)):
                    self.copy(self.k2[:, i, kt * TT:(kt + 1) * TT], self.ps[b][:], [("ps", b)], [("k2", i, kt)], eng="dve")
            for c in range(nkt * 4):
                b = self.bank()
                for i in range(2):
                    for kc in range(2):
                        self.mm(b, self.ps[b][:, i * 128:(i + 1) * 128], self.ckv[:, kc, c * 128:(c + 1) * 128],
                                wkp[:, kc, i * 256 + 128:i * 256 + 256], kc == 0, kc == 1, [("ws", sk_), ("ckv", kc, c // 4)])
                self.copy(self.v2[:, c, :], self.ps[b][:, 0:256], [("ps", b)], [("v2", c)])
            if self.cfg.get("mla_stop", 9) <= 1:
                continue
            self.attention(gi, pr % 2, [(0, 0, 0, None), (1, 1, 1, None)], sc_mla, True)
            if pr % 2 == 1:
                self.wout_group(l, (pr - 1) * 256)
        if self.cfg.get("mix_stop", 9) <= 2:
            return
        if gi == 1:
            self.join([("cqn", oc, t) for oc in range(4) for t in range(NTT)], ["ez"])
            self.build_ez(l)
        for pr in range(2):
            sq_, wqp = self.wload(winv[:, :, S3 + pr * 256:S3 + (pr + 1) * 256], [128, KC, 256])
            sk_, wkp = self.wload(winv[:, :, S3 + 512 + pr * 256:S3 + 512 + (pr + 1) * 256], [128, KC, 256])
            sv_, wvp = self.wload(winv[:, :, S3 + 1024 + pr * 256:S3 + 1024 + (pr + 1) * 256], [128, KC, 256])
            if gi == 1:
                for i in range(2):
                    self.dma("pool", self.k2[:, i, 0:PAST], self.c_nakT[l, 2 * pr + i], writes=[("k2", i, 0)])
                self.dma("pool", self.v2[:, 0:4, :], self.c_nav[l].rearrange("(c p) f -> p c f", p=128)[:, :, pr * 256:(pr + 1) * 256],
                         writes=[("v2", c) for c in range(4)])
            for i in range(2):
                for (t, b) in self.projA(sq_, wqp, i * 128, 128, self.hrhs, KC, range(NTT)):
                    self.copy(self.q2[:, i, t * TT:(t + 1) * TT], self.ps[b][:], [("ps", b)], [("q2", i, t)])
                for (t, b) in self.projA(sk_, wkp, i * 128, 128, self.hrhs, KC, range(NTT)):
                    kt = (koff + t * TT) // TT
                    if gi == 0:
                        fr, fa = self.ft()
                        self.copy(fa, self.ps[b][:], [("ps", b)], [fr], eng="dve")
                        self.copy(self.k2[:, i, koff + t * TT:koff + (t + 1) * TT], fa, [fr], [("k2", i, kt)])
                        self.dma("sp", self.o_nakT[l, 2 * pr + i][:, t * TT:(t + 1) * TT], fa, reads=[fr])
                    else:
                        self.copy(self.k2[:, i, koff + t * TT:koff + (t + 1) * TT], self.ps[b][:], [("ps", b)], [("k2", i, kt)])
            if self.cfg.get("na_stop", 9) <= 1:
                continue
            self.vproj(l, gi, sv_, wvp, kc0, self.o_nav, pr * 256)
            if self.cfg.get("na_stop", 9) <= 2:
                continue
            self.attention(gi, pr % 2, [(i, i, i, (2 * pr + i) if gi == 1 else None) for i in range(2)], sc, False)
            if pr % 2 == 1:
                self.wout_group(l, 1024)
        if self.cfg.get("mix_stop", 9) <= 3:
            return
        sk_, wkp = self.wload(winv[:, :, S5:S5 + 256], [128, KC, 256])
        sv_, wvp = self.wload(winv[:, :, S5 + 256:S5 + 512], [128, KC, 256])
        if gi == 1:
            for i in range(2):
                self.dma("pool", self.k2[:, i, 0:PAST], self.c_gkT[l, i], writes=[("k2", i, 0)])
            self.dma("pool", self.v2[:, 0:4, :], self.c_gv[l].rearrange("(c p) f -> p c f", p=128), writes=[("v2", c) for c in range(4)])

        def normed(b, gcol, t, out_bf, out_res, out_dram):
            r2r, r2 = self.ft()
            self.rstd_bcast([(self.ps[b][:], [("ps", b)])], 128, r2, r2r)
            fr, fa = self.ft()
            self.stt(fa, self.ps[b][:], self.vec[:, gcol:gcol + 1], r2, ALU.mult, ALU.mult, [("ps", b), r2r, "vec"], [fr])
            if gi == 0:
                self.copy(out_bf, fa, [fr], out_res)
                if out_dram is not None:
                    self.dma("sp", out_dram, fa, reads=[fr])
            else:
                self.rope(fa, fr, 128, t, out_bf, out_res)

        for i in range(2):
            for (t, b) in self.projA(sk_, wkp, i * 128, 128, self.hrhs, KC, range(NTT)):
                kt = (koff + t * TT) // TT
                normed(b, vb + VB_GKN, t, self.k2[:, i, koff + t * TT:koff + (t + 1) * TT], [("k2", i, kt)],
                       self.o_gkT[l, i][:, t * TT:(t + 1) * TT] if gi == 0 else None)
        self.vproj(l, gi, sv_, wvp, kc0, self.o_gv, 0)
        for pr in range(2):
            sq_, wqp = self.wload(winv[:, :, S4 + pr * 256:S4 + (pr + 1) * 256], [128, KC, 256])
            for i in range(2):
                for (t, b) in self.projA(sq_, wqp, i * 128, 128, self.hrhs, KC, range(NTT)):
                    normed(b, vb + VB_GQN, t, self.q2[:, i, t * TT:(t + 1) * TT], [("q2", i, t)], None)
            self.attention(gi, pr % 2, [(i, pr, pr, None) for i in range(2)], sc, False)
            if pr % 2 == 1:
                self.wout_group(l, 1536)


def _fm(a):
    t, d = a.shape
    return np.ascontiguousarray(a.T.reshape(d // 128, 128, t).transpose(1, 0, 2))


def _fm_inv(a):
    p, kc, t = a.shape
    return np.ascontiguousarray(a.transpose(1, 0, 2).reshape(kc * p, t).T)


def _vcols(v):
    return np.asarray(v, np.float32).reshape(-1, 128).T


def _rope_consts():
    def tables(d):
        q = d // 4
        t = np.arange(NT)
        pos = np.stack([t // 64, t % 64], axis=0).astype(np.float32)
        inv = (np.float32(10000.0) ** (-np.arange(q, dtype=np.float32) / np.float32(q))).astype(np.float32)
        cos = np.zeros((d, NT), np.float32)
        sin = np.zeros((d, NT), np.float32)
        rt = np.zeros((d, d), np.float32)
        for p in range(d):
            blk, which, j = p // (2 * q), (p // q) % 2, p % q
            ang = (pos[blk] * inv[j]).astype(np.float32)
            cos[p] = np.cos(ang)
            sin[p] = np.sin(ang)
            if which == 0:
                rt[p + q, p] = -1.0
            else:
                rt[p - q, p] = 1.0
        return cos, sin, rt
    c128, s128, r128 = tables(128)
    c64, s64, r64 = tables(64)
    rot = np.zeros((128, 192), np.float32)
    rot[:, 0:128] = r128
    rot[0:64, 128:192] = r64
    w1 = np.zeros((31, 127), np.float32)
    for dc in range(31):
        w1[dc, dc + 48] = 1.0
    vc = np.zeros((64, 64), np.float32)
    for qc in range(64):
        cs = min(max(qc - 8, 0), 48)
        vc[cs:cs + 16, qc] = 1.0
    return {"cos128": c128, "sin128": s128, "cos64": c64, "sin64": s64, "rotT": rot, "w1": w1, "validc": vc}


def prepare_inputs(inp, cfg, ncores=NCORES):
    f = lambda k: np.asarray(inp[k], dtype=np.float32)
    vec_l = []
    for l in range(L):
        vec_l += [_vcols(f("ada_b")[l]), _vcols(f("norm_g")[l].reshape(-1)), _vcols(f("mla_q_norm")[l]),
                  _vcols(f("mla_kv_norm")[l]), _vcols(f("gqa_q_norm")[l]), _vcols(f("gqa_k_norm")[l])]
    vec_l.append(_vcols(f("final_norm")))
    vecs = np.ascontiguousarray(np.concatenate(vec_l, axis=1))
    shared = {"vecs": vecs}
    for k in ("ada_w", "ffn_wg", "ffn_wu", "ffn_wd", "w_in", "mla_wqb", "mla_wkvb", "w_out"):
        shared[k] = f(k)
    shared.update(_rope_consts())
    rpb = f("na_rpb")
    shared["rpbT"] = np.ascontiguousarray(rpb[:, :, ::-1, :].transpose(3, 0, 1, 2).reshape(31, L, 60))
    xp, xs, c, cctx = f("x_prompt"), f("x_sample"), f("c"), f("c_ctx")
    ckv, krp = f("cache_mla_ckv"), f("cache_mla_krope")
    nak, nav, gk, gv = f("cache_na_k"), f("cache_na_v"), f("cache_gqa_k"), f("cache_gqa_v")
    in_maps = []
    for k in range(ncores):
        m = dict(shared)
        m["xT_p"] = _fm(xp[4 * k:4 * k + 4].reshape(NT, D))
        m["xT_s"] = _fm(xs[k])
        m["condT"] = np.ascontiguousarray(np.stack([_vcols(cctx), _vcols(c[k])], axis=-1))
        m["c_ckvT"] = np.ascontiguousarray(ckv[k].transpose(0, 2, 1).reshape(L, 2, 128, PAST).transpose(0, 2, 1, 3))
        m["c_krT"] = np.ascontiguousarray(krp[k].transpose(0, 2, 1))
        m["c_nakT"] = np.ascontiguousarray(nak[k].transpose(0, 2, 3, 1))
        m["c_nav"] = np.ascontiguousarray(nav[k].reshape(L, PAST, 512))
        m["c_gkT"] = np.ascontiguousarray(gk[k].transpose(0, 2, 3, 1))
        m["c_gv"] = np.ascontiguousarray(gv[k].reshape(L, PAST, 256))
        in_maps.append(m)
    return in_maps


def assemble(outs):
    n = len(outs)
    y_prompt = np.stack([_fm_inv(o["yT_p"]) for o in outs]).reshape(4 * n, 256, D)
    y_sample = np.stack([_fm_inv(o["yT_s"]) for o in outs])
    def per_seq(a):
        nn, l, t, fdim = a.shape
        return np.ascontiguousarray(a.reshape(nn, l, 4, 256, fdim).transpose(0, 2, 1, 3, 4).reshape(nn * 4, l, 256, fdim))
    ckv = np.stack([o["o_ckvT"].transpose(0, 2, 1, 3).reshape(L, 256, NT).transpose(0, 2, 1) for o in outs])
    kr = np.stack([o["o_krT"].transpose(0, 2, 1) for o in outs])
    nak = np.stack([o["o_nakT"].transpose(0, 3, 1, 2).reshape(L, NT, 512) for o in outs])
    nav = np.stack([o["o_nav"] for o in outs])
    gk = np.stack([o["o_gkT"].transpose(0, 3, 1, 2).reshape(L, NT, 256) for o in outs])
    gv = np.stack([o["o_gv"] for o in outs])
    return (y_prompt, y_sample, per_seq(ckv), per_seq(kr),
            per_seq(nak).reshape(4 * n, L, 256, 4, 128), per_seq(nav).reshape(4 * n, L, 256, 4, 128),
            per_seq(gk).reshape(4 * n, L, 256, 2, 128), per_seq(gv).reshape(4 * n, L, 256, 2, 128))


_CACHE = {}


def run(inputs, cfg, ncores=NCORES, trace=False):
    key = tuple(sorted(cfg.items()))
    if key not in _CACHE:
        _CACHE[key] = Builder(dict(cfg)).build()
    nc = _CACHE[key]
    in_maps = prepare_inputs(inputs, cfg, ncores)
    res = run_bass_kernel_spmd(nc, in_maps, core_ids=list(range(ncores)), trace=trace)
    if trace:
        print("exec_time_ns", res.exec_time_ns)
    return res.results


def kernel(**inputs):
    cfg = {"layers": 2, "ffn1": True, "mixer": True, "ffn2": True}
    outs = run(inputs, cfg)
    return tuple(np.ascontiguousarray(a, dtype=np.float32) for a in assemble(outs))
```

```python
import numpy as np
from contextlib import ExitStack
import concourse.bass as bass
import concourse.mybir as mybir
from concourse.bass_utils import run_bass_kernel_spmd

F32 = mybir.dt.float32
BF16 = mybir.dt.bfloat16
AF = mybir.ActivationFunctionType
ALU = mybir.AluOpType

NCORES = 8
D = 2048
KC = 16
DFF = 5632
NFC = 44
NT = 1024
TT = 512
NTT = NT // TT
L = 2
EPS = 1e-6
PAST = 512
NS = 4
NFP = 8
NBP = 8
OVN = 23040
SLOT = 4096
S1, S2, S3 = 512, 768, 832
S4 = S3 + 1536
S5 = S4 + 512
INC = S5 + 512
NVL = 200
VB_ADAB, VB_NG, VB_QN, VB_KVN, VB_GQN, VB_GKN = 0, 144, 192, 196, 198, 199


class Op:
    __slots__ = ("eng", "fn", "deps", "sig", "rank", "dma", "dsi", "dval")


class Prog:
    def __init__(self, ndsem=8):
        self.ops = []
        self.lastw = {}
        self.rd_eng = {}
        self.rd_dma = {}
        self.ndsem = ndsem
        self.dq_cnt = {"pool": [0] * ndsem, "sp": [0] * ndsem}
        self.dq_last = {"pool": [None] * ndsem, "sp": [None] * ndsem}
        self.dq_rr = {"pool": 0, "sp": 0}

    def add(self, eng, fn, reads=(), writes=(), dma=False):
        idx = len(self.ops)
        op = Op()
        op.eng, op.fn, op.sig, op.rank, op.dma = eng, fn, False, 0, dma
        deps = set()
        for r in reads:
            w = self.lastw.get(r)
            if w is not None:
                deps.add(w)
        for r in writes:
            w = self.lastw.get(r)
            if w is not None:
                deps.add(w)
            for ri in self.rd_eng.get(r, {}).values():
                deps.add(ri)
            for ri in self.rd_dma.get(r, ()):
                deps.add(ri)
        if dma:
            k = self.dq_rr[eng] % self.ndsem
            self.dq_rr[eng] += 1
            prev = self.dq_last[eng][k]
            if prev is not None:
                deps.add(prev)
            self.dq_cnt[eng][k] += 1
            self.dq_last[eng][k] = idx
            op.dsi = k
            op.dval = 16 * self.dq_cnt[eng][k]
        for r in reads:
            if dma:
                self.rd_dma.setdefault(r, []).append(idx)
            else:
                self.rd_eng.setdefault(r, {})[eng] = idx
        for r in writes:
            self.lastw[r] = idx
            self.rd_eng[r] = {}
            self.rd_dma[r] = []
        deps.discard(idx)
        op.deps = deps
        self.ops.append(op)
        return idx

    def emit(self, nc):
        ops = self.ops
        for op in ops:
            for d in op.deps:
                dop = ops[d]
                if not dop.dma and not (dop.eng == "pe" and op.eng == "pe"):
                    dop.sig = True
        cnt = {}
        for op in ops:
            if not op.dma and op.sig:
                cnt[op.eng] = cnt.get(op.eng, 0) + 1
                op.rank = cnt[op.eng]
        with ExitStack() as es:
            esem = {e: es.enter_context(nc.semaphore("s_" + e)) for e in ("pe", "act", "dve", "pool")}
            dsem = {q: [es.enter_context(nc.semaphore("d_%s%d" % (q, i))) for i in range(self.ndsem)]
                    for q in ("pool", "sp")}
            block = es.enter_context(nc.Block())

            def make(eng_name):
                def body(e):
                    seen = {}
                    for op in ops:
                        if op.eng != eng_name:
                            continue
                        need = {}
                        for d in op.deps:
                            dop = ops[d]
                            if dop.dma:
                                key = ("d", dop.eng, dop.dsi)
                                val = dop.dval
                            else:
                                if dop.eng == "pe" and eng_name == "pe":
                                    continue
                                key = ("e", dop.eng)
                                val = dop.rank
                            if val > need.get(key, 0):
                                need[key] = val
                        for key, val in need.items():
                            if seen.get(key, 0) >= val:
                                continue
                            seen[key] = val
                            s = dsem[key[1]][key[2]] if key[0] == "d" else esem[key[1]]
                            e.wait_ge(s, val)
                        ins = op.fn(e)
                        if op.dma:
                            ins.then_inc(dsem[eng_name][op.dsi], 16)
                        elif op.sig:
                            ins.then_inc(esem[eng_name], 1)
                    if eng_name in ("pool", "sp"):
                        for k in range(self.ndsem):
                            if self.dq_cnt[eng_name][k]:
                                e.wait_ge(dsem[eng_name][k], 16 * self.dq_cnt[eng_name][k])
                return body

            block.tensor(make("pe"))
            block.scalar(make("act"))
            block.vector(make("dve"))
            block.gpsimd(make("pool"))
            block.sync(make("sp"))


class Builder:
    def __init__(self, cfg):
        self.cfg = cfg
        self.nc = bass.Bass("TRN2", target_bir_lowering=False)
        self.P = Prog()
        self.es = ExitStack()
        self.bank_i = 0
        self.ring_i = 0
        self.sq_i = 0
        self.held = set()
        self.ovl_owner = "ffn"
        self.ft_i = 0
        self.bt_i = 0

    def dram_in(self, name, shape, dt=F32):
        return self.nc.dram_tensor(name, list(shape), dt, kind="ExternalInput").ap()

    def dram_out(self, name, shape, dt=F32):
        return self.nc.dram_tensor(name, list(shape), dt, kind="ExternalOutput").ap()

    def sb(self, name, shape, dt):
        return self.es.enter_context(self.nc.sbuf_tensor(name, list(shape), dt))

    def bank(self, hold=False):
        while True:
            b = self.bank_i % 8
            self.bank_i += 1
            if b not in self.held:
                break
        if hold:
            self.held.add(b)
        return b

    def unhold(self, b):
        self.held.discard(b)

    def ft(self):
        i = self.ft_i % NFP
        self.ft_i += 1
        return ("fb", i), self.fpool[:, i, :]

    def bt(self):
        i = self.bt_i % NBP
        self.bt_i += 1
        return ("bb", i), self.bpool[:, i, :]

    def join(self, reads, writes):
        self.P.add("dve", lambda e: e.memset(self.jscr[:], 0.0), reads=list(reads) + ["jscr"], writes=list(writes) + ["jscr"])

    def wload(self, src, shape):
        lst = self.ring_ffn if self.ovl_owner == "ffn" else self.ring_mix
        s = lst[self.ring_i % len(lst)]
        self.ring_i += 1
        n = int(np.prod(shape[1:]))
        assert n <= SLOT, shape
        flat = self.ws[s][:, 0:n]
        if len(shape) == 3:
            view = flat.rearrange("p (a b) -> p a b", a=shape[1])
        else:
            view = flat
        self.P.add("pool", lambda e, o=self.split256(view), i=self.split256(src): e.dma_start(out=o, in_=i),
                   writes=[("ws", s)], dma=True)
        return s, view

    def mm(self, b, out, lhsT, rhs, start, stop, reads):
        self.P.add("pe", lambda e, o=out, l=lhsT, r=rhs, s0=start, s1=stop:
                   e.matmul(o, l, r, start=s0, stop=s1), reads=reads, writes=[("ps", b)])

    def build(self):
        nc, P, cfg = self.nc, self.P, self.cfg
        nl = cfg.get("layers", L)
        self.xT = {g: self.dram_in("xT_" + g, [128, KC, NT]) for g in "ps"}
        self.condT = self.dram_in("condT", [128, KC, 2])
        self.vecs = self.dram_in("vecs", [128, L * NVL + 16])
        self.ada_w = self.dram_in("ada_w", [L, D, 9 * D])
        self.wg = self.dram_in("ffn_wg", [L, 2, D, DFF])
        self.wu = self.dram_in("ffn_wu", [L, 2, D, DFF])
        self.wd = self.dram_in("ffn_wd", [L, 2, DFF, D])
        self.yT = {g: self.dram_out("yT_" + g, [128, KC, NT]) for g in "ps"}
        self.x = self.sb("x", [128, KC, NT], F32)
        self.h = self.sb("h", [128, KC, NT], BF16)
        self.ws = [self.sb("ws%d" % i, [128, SLOT], BF16) for i in range(NS)]
        self.ps = [self.es.enter_context(nc.psum_tensor("ps%d" % i, [128, TT], F32)) for i in range(8)]
        self.vec = self.sb("vec", [128, L * NVL + 16], F32)
        self.cond = self.sb("cond", [128, KC, 2], F32)
        self.scond = self.sb("scond", [128, KC, 2], BF16)
        self.modt = [self.sb("modt%d" % l, [128, 144, 2], F32) for l in range(L)]
        self.gs = self.sb("gs", [128, 3, KC], F32)
        self.gh = self.sb("gh", [128, 3, KC], F32)
        self.ones = self.sb("ones", [128, 128], BF16)
        self.rstd = self.sb("rstd", [128, TT], F32)
        self.fpool = self.sb("fpool", [128, NFP, TT], F32)
        self.bpool = self.sb("bpool", [128, NBP, TT], BF16)
        self.jscr = self.sb("jscr", [128, 2], F32)
        self.epsb = self.sb("epsb", [128, 2], F32)
        self.msb = self.sb("msb", [2, 2, 256], F32)
        self.id2 = self.sb("id2s", [2, 2], F32)
        self.d_id2 = self.dram_in("id2", [2, 2])
        self.mod_pending = None
        self.ovl = self.sb("ovl", [128, OVN], BF16)
        self.abuf = self.ovl[:, 0:12 * NT].rearrange("p (c t) -> p c t", c=12)
        self.ws = self.ws + [self.ovl[:, 12288:12288 + SLOT], self.ovl[:, 16384:16384 + SLOT]]
        self.ring_ffn = [0, 1, 2, 3, 4, 5]
        self.ring_mix = [0, 1, 2, 3]
        self.mixer_setup()
        self.nl = nl
        self.mod_bank = self.bank(hold=True)
        self.mod_next = 0
        self.mod_total = nl * 72

        P.add("pool", lambda e: e.memset(self.ones[:], 1.0), writes=["ones"])
        P.add("pool", lambda e: e.memset(self.epsb[:], EPS), writes=["epsb"])
        P.add("sp", lambda e: e.dma_start(out=self.id2[:], in_=self.d_id2), writes=["id2"], dma=True)
        P.add("sp", lambda e: e.dma_start(out=self.vec[:], in_=self.vecs), writes=["vec"], dma=True)
        P.add("sp", lambda e: e.dma_start(out=self.cond[:], in_=self.condT), writes=["cond"], dma=True)
        P.add("act", lambda e: e.activation(out=self.scond[:], in_=self.cond[:], func=AF.Silu),
              reads=["cond"], writes=["scond"])
        for gi, g in enumerate("ps"):
            if g not in cfg.get("groups", "ps"):
                continue
            self.load_x(g)
            for l in range(nl):
                if cfg.get("ffn1", True):
                    self.ffn(l, 0, gi)
                if cfg.get("mixer", False):
                    self.mixer(l, gi)
                if cfg.get("ffn2", True):
                    self.ffn(l, 1, gi)
            self.final_norm(g)
        if cfg.get("dbg"):
            for name, (buf, shape, dt, res) in self.dbg_bufs().items():
                o = self.dram_out("dbg_" + name, shape, dt)
                P.add("sp", lambda e, o=o, buf=buf: e.dma_start(out=o, in_=buf), reads=res, dma=True)
        P.emit(nc)
        return nc

    def dbg_bufs(self):
        return {
            "x": (self.x[:], [128, KC, NT], F32, [("x", kc, t) for kc in range(KC) for t in range(NTT)]),
            "h": (self.h[:], [128, KC, NT], BF16, [("h", kc, t) for kc in range(KC) for t in range(NTT)]),
            "ovl": (self.ovl[:], [128, OVN], BF16, self.mix_res() + ["ez"]),
        }

    def ffn_res(self):
        return [("a", j, t) for j in range(12) for t in range(NTT)] + [("ws", 4), ("ws", 5)]

    def act(self, out, in_, func, reads, writes, **kw):
        self.P.add("act", lambda e, o=out, i=in_, f=func, kw=kw: e.activation(out=o, in_=i, func=f, **kw),
                   reads=reads, writes=writes)

    def tt(self, out, in0, in1, op, reads, writes, eng="dve"):
        self.P.add(eng, lambda e, o=out, a=in0, b=in1, op=op: e.tensor_tensor(out=o, in0=a, in1=b, op=op),
                   reads=reads, writes=writes)

    def ts(self, out, in0, s1, s2, op0, op1, reads, writes, eng="dve"):
        if op1 is None:
            self.P.add(eng, lambda e, o=out, a=in0, s1=s1, op0=op0: e.tensor_scalar(
                out=o, in0=a, scalar1=s1, scalar2=None, op0=op0), reads=reads, writes=writes)
        else:
            self.P.add(eng, lambda e, o=out, a=in0, s1=s1, s2=s2, op0=op0, op1=op1: e.tensor_scalar(
                out=o, in0=a, scalar1=s1, scalar2=s2, op0=op0, op1=op1), reads=reads, writes=writes)

    def stt(self, out, in0, scalar, in1, op0, op1, reads, writes, eng="dve"):
        self.P.add(eng, lambda e, o=out, a=in0, s=scalar, b=in1, op0=op0, op1=op1: e.scalar_tensor_tensor(
            out=o, in0=a, scalar=s, in1=b, op0=op0, op1=op1), reads=reads, writes=writes)

    def recip(self, out, in_, reads, writes):
        self.P.add("dve", lambda e, o=out, i=in_: e.reciprocal(out=o, in_=i), reads=reads, writes=writes)

    def dma(self, q, out, in_, reads=(), writes=()):
        if q == "pool":
            out, in_ = self.split256(out), self.split256(in_)
        self.P.add(q, lambda e, o=out, i=in_: e.dma_start(out=o, in_=i), reads=reads, writes=writes, dma=True)

    @staticmethod
    def split256(ap):
        n = ap.shape[-1]
        if n <= 256 or n % 256:
            return ap
        nd = len(ap.shape)
        if nd == 2:
            return ap.rearrange("p (a b) -> p a b", b=256)
        if nd == 3:
            return ap.rearrange("p c (a b) -> p c a b", b=256)
        return ap

    def mod_job(self):
        self.mod_flush_pending()
        j = self.mod_next
        self.mod_next += 1
        l, ti = divmod(j, 72)
        av = self.ada_w[l].rearrange("(kc p) m -> p kc m", p=128)
        s, wv = self.wload(av[:, :, ti * 256:(ti + 1) * 256], [128, KC, 256])
        ba = self.bank()
        for kc in range(KC):
            self.mm(ba, self.ps[ba][0:2, 0:256], self.scond[:, kc, :], wv[:, kc, :], kc == 0, kc == KC - 1,
                    [("ws", s), "scond"])
        q = j % 2
        self.copy(self.msb[0:2, q, :], self.ps[ba][0:2, 0:256], [("ps", ba)], [("msb", q)], eng="dve")
        self.mod_pending = (l, ti, q, j)

    def mod_flush_pending(self):
        if self.mod_pending is None:
            return
        (l, ti, q, j) = self.mod_pending
        self.mod_pending = None
        i = ti // 8
        b = self.mod_bank
        for m in range(2):
            ocl = (i % 2) * 16 + (ti % 8) * 2 + m
            self.mm(b, self.ps[b][:, ocl * 2:ocl * 2 + 2], self.msb[0:2, q, m * 128:(m + 1) * 128], self.id2[0:2, 0:2],
                    True, True, [("msb", q), "id2"])
        if ti % 8 == 7:
            base = l * NVL + VB_ADAB + i * 16
            c0 = (i % 2) * 32
            mps = self.ps[b][:, c0:c0 + 32].rearrange("p (c s) -> p c s", s=2)
            for s_ in range(2):
                self.tt(self.modt[l][:, i * 16:(i + 1) * 16, s_], mps[:, :, s_], self.vec[:, base:base + 16], ALU.add,
                        [("ps", b), "vec"], [("modt", l, i)])
        if j == self.mod_total - 1:
            self.unhold(self.mod_bank)

    def mod_need(self, l, i3):
        target = min(l * 72 + (3 * i3 + 3) * 8, self.mod_total)
        while self.mod_next < target:
            self.mod_job()
        self.mod_flush_pending()

    def bg(self, n=1):
        for _ in range(n):
            if self.mod_next < self.mod_total:
                self.mod_job()

    def load_x(self, g):
        for kc4 in range(0, KC, 4):
            self.dma("sp", self.x[:, kc4:kc4 + 4, :], self.xT[g][:, kc4:kc4 + 4, :],
                     writes=[("x", kc, t) for kc in range(kc4, kc4 + 4) for t in range(NTT)])

    def prep_mods(self, l, gi, i):
        target = min(l * 72 + (3 * i + 2) * 8, self.mod_total)
        while self.mod_next < target:
            self.mod_job()
        self.mod_flush_pending()
        m = self.modt[l]
        sc = m[:, (3 * i + 1) * 16:(3 * i + 2) * 16, gi]
        ng = self.vec[:, l * NVL + VB_NG + i * 16: l * NVL + VB_NG + (i + 1) * 16]
        self.stt(self.gs[:, i, :], sc, 1.0, ng, ALU.add, ALU.mult, [("modt", l, 3 * i + 1), "vec"], [("gs", i)])

    def prep_gate(self, l, gi, i):
        self.mod_need(l, i)
        m = self.modt[l]
        gt = m[:, (3 * i + 2) * 16:(3 * i + 3) * 16, gi]
        self.ts(self.gh[:, i, :], gt, (1.0 if i == 1 else 0.5), None, ALU.mult, None, [("modt", l, 3 * i + 2)], [("gh", i)])

    def rstd_bcast(self, srcs, nfeat, rstd, rres):
        b = self.bank()
        n = srcs[0][0].shape[-1]
        for k, (ap, rd) in enumerate(srcs):
            qr, qa = self.bt()
            self.act(qa[:, 0:n], ap, AF.Square, rd, [qr])
            self.mm(b, self.ps[b][:, 0:n], self.ones[:], qa[:, 0:n], k == 0, k == len(srcs) - 1,
                    [qr, "ones"])
        self.act(rstd, self.ps[b][:, 0:n], AF.Ln, [("ps", b), "epsb"], [rres], scale=1.0 / nfeat, bias=self.epsb[:, 0:1])
        self.act(rstd, rstd, AF.Exp, [rres], [rres], scale=-0.5)

    def rstd_from_x(self, t):
        tsl = slice(t * TT, (t + 1) * TT)
        self.rstd_bcast([(self.x[:, kc, tsl], [("x", kc, t)]) for kc in range(KC)], D, self.rstd[:], "rstd")

    def modnorm(self, l, i, gi):
        m = self.modt[l]
        for t in range(NTT):
            tsl = slice(t * TT, (t + 1) * TT)
            self.rstd_from_x(t)
            for kc in range(KC):
                nr, na = self.ft()
                self.tt(na, self.x[:, kc, tsl], self.rstd[:], ALU.mult,
                        [("x", kc, t), "rstd"], [nr])
                sh = m[:, 3 * i * 16 + kc, gi:gi + 1]
                self.act(self.h[:, kc, tsl], na, AF.Identity,
                         [nr, ("gs", i), ("modt", l, 3 * i)], [("h", kc, t)],
                         scale=self.gs[:, i, kc:kc + 1], bias=sh)

    def ffn(self, l, fi, gi):
        i = 0 if fi == 0 else 2
        if self.ovl_owner != "ffn":
            self.join(self.mix_res() + ["ez"], self.ffn_res())
            self.ovl_owner = "ffn"
            self.ring_i = 0
        self.prep_mods(l, gi, i)
        self.modnorm(l, i, gi)
        wgv = self.wg[l, fi].rearrange("(kc p) m -> p kc m", p=128)
        wuv = self.wu[l, fi].rearrange("(kc p) m -> p kc m", p=128)
        wdv = self.wd[l, fi].rearrange("(c p) m -> p c m", p=128)
        segs = [(0, 12), (12, 12), (24, 10), (34, 10)]
        for (c0, n) in segs:
            for jt in range(n // 2):
                col = (c0 + 2 * jt) * 128
                sg_, gv = self.wload(wgv[:, :, col:col + 256], [128, KC, 256])
                su_, uv = self.wload(wuv[:, :, col:col + 256], [128, KC, 256])
                for t in range(NTT):
                    tsl = slice(t * TT, (t + 1) * TT)
                    for m in range(2):
                        jl = 2 * jt + m
                        bg = self.bank()
                        bu = self.bank()
                        for kc in range(KC):
                            self.mm(bg, self.ps[bg][:], gv[:, kc, m * 128:(m + 1) * 128], self.h[:, kc, tsl],
                                    kc == 0, kc == KC - 1, [("ws", sg_), ("h", kc, t)])
                        for kc in range(KC):
                            self.mm(bu, self.ps[bu][:], uv[:, kc, m * 128:(m + 1) * 128], self.h[:, kc, tsl],
                                    kc == 0, kc == KC - 1, [("ws", su_), ("h", kc, t)])
                        sr, sa = self.ft()
                        self.act(sa, self.ps[bg][:], AF.Silu, [("ps", bg)], [sr])
                        self.tt(self.abuf[:, jl, tsl], self.ps[bu][:], sa, ALU.mult,
                                [("ps", bu), sr], [("a", jl, t)])
                self.bg(1)
            if c0 == 0:
                self.prep_gate(l, gi, i)
            for ot in range(8):
                sd_, dv = self.wload(wdv[:, c0:c0 + n, ot * 256:(ot + 1) * 256], [128, n, 256])
                for m in range(2):
                    oc = 2 * ot + m
                    for t in range(NTT):
                        tsl = slice(t * TT, (t + 1) * TT)
                        b = self.bank()
                        for c in range(n):
                            self.mm(b, self.ps[b][:], dv[:, c, m * 128:(m + 1) * 128], self.abuf[:, c, tsl],
                                    c == 0, c == n - 1, [("ws", sd_), ("a", c, t)])
                        self.stt(self.x[:, oc, tsl], self.ps[b][:], self.gh[:, i, oc:oc + 1], self.x[:, oc, tsl],
                                 ALU.mult, ALU.add, [("ps", b), ("gh", i), ("x", oc, t)], [("x", oc, t)])
                self.bg(1)

    def final_norm(self, g):
        fb = L * NVL
        for t in range(NTT):
            tsl = slice(t * TT, (t + 1) * TT)
            self.rstd_from_x(t)
            for kc in range(KC):
                orr, oa = self.ft()
                self.stt(oa, self.x[:, kc, tsl], self.vec[:, fb + kc:fb + kc + 1], self.rstd[:],
                         ALU.mult, ALU.mult, [("x", kc, t), "rstd", "vec"], [orr])
                self.dma("sp", self.yT[g][:, kc, tsl], oa, reads=[orr])

    def mixer_setup(self):
        nc = self.nc
        o = self.ovl
        def v3(off, a, b):
            return o[:, off:off + a * b].rearrange("p (a b) -> p a b", a=a)
        self.q2 = v3(0, 2, NT)
        self.k2 = v3(2048, 2, 1536)
        self.v2 = v3(5120, 12, 256)
        self.O2 = v3(8192, 2, NT)
        self.Ob = [self.O2, v3(20992, 2, NT)]
        assert 20992 + 2 * NT <= OVN
        self.qr2 = v3(10240, 2, NT)
        self.cqn = v3(12288, 4, NT)
        self.ez = o[:, 12288:12288 + 4096].rearrange("p (h u q) -> p h u q", h=4, u=16)
        self.ckv = v3(16384, 2, 1536)
        self.kr = o[:, 19456:19456 + 1536]
        assert 19456 + 1536 <= OVN
        self.w_in = self.dram_in("w_in", [L, D, INC])
        self.wqb = self.dram_in("mla_wqb", [L, 512, 1536])
        self.wkvb = self.dram_in("mla_wkvb", [L, 256, 2048])
        self.w_out = self.dram_in("w_out", [L, D, D])
        self.c_ckvT = self.dram_in("c_ckvT", [L, 128, 2, PAST])
        self.c_krT = self.dram_in("c_krT", [L, 64, PAST])
        self.c_nakT = self.dram_in("c_nakT", [L, 4, 128, PAST])
        self.c_nav = self.dram_in("c_nav", [L, PAST, 512])
        self.c_gkT = self.dram_in("c_gkT", [L, 2, 128, PAST])
        self.c_gv = self.dram_in("c_gv", [L, PAST, 256])
        self.d_cos128 = self.dram_in("cos128", [128, NT])
        self.d_sin128 = self.dram_in("sin128", [128, NT])
        self.d_cos64 = self.dram_in("cos64", [64, NT])
        self.d_sin64 = self.dram_in("sin64", [64, NT])
        self.d_rot = self.dram_in("rotT", [128, 192])
        self.d_w1 = self.dram_in("w1", [31, 127])
        self.d_vc = self.dram_in("validc", [64, 64])
        self.d_rpbT = self.dram_in("rpbT", [31, L, 60])
        self.o_ckvT = self.dram_out("o_ckvT", [L, 128, 2, NT])
        self.o_krT = self.dram_out("o_krT", [L, 64, NT])
        self.o_nakT = self.dram_out("o_nakT", [L, 4, 128, NT])
        self.o_nav = self.dram_out("o_nav", [L, NT, 512])
        self.o_gkT = self.dram_out("o_gkT", [L, 2, 128, NT])
        self.o_gv = self.dram_out("o_gv", [L, NT, 256])
        self.rot = self.sb("rot", [128, 192], F32)
        self.w1 = self.sb("w1s", [31, 127], F32)
        self.vc = self.sb("vcs", [64, 64], BF16)
        self.rpb = self.sb("rpbs", [31, L, 60], F32)
        self.dma("sp", self.rot[:], self.d_rot, writes=["rot"])
        self.dma("sp", self.w1[:], self.d_w1, writes=["nac"])
        self.dma("pool", self.vc[:], self.d_vc, writes=["vc"])
        self.dma("sp", self.rpb[:], self.d_rpbT, writes=["nac2"])

    def mix_res(self):
        r = []
        for i in range(2):
            for t in range(NTT):
                r += [("q2", i, t), ("qr2", i, t), ("O2", 0, i, t), ("O2", 1, i, t)]
            for kt in range(3):
                r += [("k2", i, kt), ("ckv", i, kt)]
        r += [("kr", kt) for kt in range(3)]
        r += [("v2", c) for c in range(12)]
        r += [("cqn", oc, t) for oc in range(4) for t in range(NTT)]
        return r

    def copy(self, out, in_, reads, writes, eng="act"):
        if eng == "act":
            self.P.add("act", lambda e, o=out, i=in_: e.copy(out=o, in_=i), reads=reads, writes=writes)
        else:
            self.P.add(eng, lambda e, o=out, i=in_: e.tensor_copy(o, i), reads=reads, writes=writes)

    def rope(self, a_ap, a_res, npart, t, out_ap, out_res):
        tsl = slice(t * TT, (t + 1) * TT)
        dc, ds = (self.d_cos128, self.d_sin128) if npart == 128 else (self.d_cos64, self.d_sin64)
        rt = self.rot[:, 0:128] if npart == 128 else self.rot[0:64, 128:192]
        cr, ca = self.ft()
        self.dma("sp", ca[0:npart, :], dc[:, tsl], writes=[cr])
        sr, sa = self.ft()
        self.dma("sp", sa[0:npart, :], ds[:, tsl], writes=[sr])
        b = self.bank()
        self.mm(b, self.ps[b][0:npart, :], rt, a_ap, True, True, [a_res, "rot"])
        t1r, t1 = self.ft()
        self.tt(t1[0:npart, :], a_ap, ca[0:npart, :], ALU.mult, [a_res, cr], [t1r])
        t2r, t2 = self.ft()
        self.tt(t2[0:npart, :], self.ps[b][0:npart, :], sa[0:npart, :], ALU.mult, [("ps", b), sr], [t2r])
        self.tt(out_ap, t1[0:npart, :], t2[0:npart, :], ALU.add, [t1r, t2r], out_res)

    def build_ez(self, l):
        for q8 in range(8):
            b = self.bank()
            for qcl in range(8):
                qc = q8 * 8 + qcl
                self.mm(b, self.ps[b][0:64, qcl * 60:(qcl + 1) * 60], self.w1[:, 63 - qc:127 - qc], self.rpb[:, l, :],
                        True, True, ["nac", "nac2"])
            pin = self.ps[b][0:64, 0:480].rearrange("p (q h u) -> p h u q", q=8, h=4)
            self.act(self.ez[0:64, :, 0:15, q8 * 8:(q8 + 1) * 8], pin, AF.Exp, [("ps", b)], ["ez"])
        for hh in range(4):
            for u in range(15):
                self.tt(self.ez[0:64, hh, u, :], self.ez[0:64, hh, u, :], self.vc[:], ALU.mult, ["ez", "vc"], ["ez"])
        self.dma("sp", self.ez[64:128, :, 1:16, :], self.ez[0:64, :, 0:15, :], reads=["ez"], writes=["ez"])

    def attention(self, gi, ob, heads, scale, mla, la=2):
        items = []
        for (i, ki, vc, na_head) in heads:
            base = dict(i=i, ki=ki, vc=vc, na_head=na_head)
            if gi == 0:
                for s_ in range(4):
                    items.append(dict(base, q0=256 * s_, qn=256, chunks=[(2 * s_, None), (2 * s_ + 1, None)], first=True, last=True))
            else:
                for qt in range(2):
                    if na_head is None:
                        ch = [(c, None) for c in range(12)]
                    else:
                        r0 = 8 * qt
                        ch = [(c, None) for c in range(4)]
                        for kr0 in (range(0, 12, 2) if qt == 0 else range(4, 16, 2)):
                            halves = []
                            for hf in range(2):
                                krow = kr0 + hf
                                rs = [r for r in range(r0, r0 + 8) if min(max(r - 4, 0), 8) <= krow <= min(max(r - 4, 0), 8) + 7]
                                halves.append((rs[0], rs[-1] + 1) if rs else None)
                            ch.append((4 + kr0 // 2, (kr0, r0, halves)))
                    for idx, cn in enumerate(ch):
                        items.append(dict(base, q0=512 * qt, qn=512, chunks=[cn], first=idx == 0, last=idx == len(ch) - 1))
        qseq = -1
        for it in items:
            if it["first"]:
                qseq += 1
            it["qseq"] = qseq
            it["gi"] = gi
        self.att_pairs = [(self.bank(hold=True), self.bank(hold=True)) for _ in range(2)]
        pend = []
        for it in items:
            self.att_S(it, scale, mla)
            pend.append(it)
            if len(pend) > la:
                self.att_PV(pend.pop(0), ob)
        for it in pend:
            self.att_PV(it, ob)
        for (b0, b1) in self.att_pairs:
            self.unhold(b0)
            self.unhold(b1)

    def att_S(self, it, scale, mla):
        i, ki, q0, qn = it["i"], it["ki"], it["q0"], it["qn"]
        t = q0 // TT
        qsl = slice(q0, q0 + qn)
        bS = self.bank()
        nch = len(it["chunks"])
        for j, (c, na) in enumerate(it["chunks"]):
            csl = slice(c * 128, (c + 1) * 128)
            kt = c // 4
            osl = slice(j * qn, (j + 1) * qn)
            self.mm(bS, self.ps[bS][:, osl], self.k2[:, ki, csl], self.q2[:, i, qsl], True, not mla,
                    [("k2", ki, kt), ("q2", i, t)])
            if mla:
                self.mm(bS, self.ps[bS][:, osl], self.kr[:, csl], self.qr2[:, i, qsl], False, True,
                        [("kr", kt), ("qr2", i, t)])
        ptr, pt = self.bt()
        (c, na) = it["chunks"][0]
        if na is None:
            self.act(pt[:, 0:nch * qn], self.ps[bS][:, 0:nch * qn], AF.Exp, [("ps", bS)], [ptr], scale=scale)
        else:
            (kr0, r0, halves) = na
            self.P.add("pool", lambda e, o=pt: e.memset(o, 0.0), writes=[ptr])
            tr, tm = self.bt()
            for hf in range(2):
                if halves[hf] is None:
                    continue
                ra, rb = halves[hf]
                psl = slice(64 * hf, 64 * hf + 64)
                fsl = slice((ra - r0) * 64, (rb - r0) * 64)
                self.act(tm[psl, fsl], self.ps[bS][psl, fsl], AF.Exp, [("ps", bS)], [tr], scale=scale)
                u0 = 7 - kr0
                ezv = self.ez[psl, it["na_head"], u0 + ra:u0 + rb, :]
                self.tt(pt[psl, fsl].rearrange("p (r q) -> p r q", q=64), tm[psl, fsl].rearrange("p (r q) -> p r q", q=64),
                        ezv, ALU.mult, [tr, "ez", ptr], [ptr])
        it["pt"] = (ptr, pt)

    def att_PV(self, it, ob):
        i, vc, q0, qn = it["i"], it["vc"], it["q0"], it["qn"]
        t = q0 // TT
        qsl = slice(q0, q0 + qn)
        bO, bD = self.att_pairs[it["qseq"] % 2]
        ptr, pt = it["pt"]
        nch = len(it["chunks"])
        for j, (c, na) in enumerate(it["chunks"]):
            psl = slice(j * qn, (j + 1) * qn)
            first = it["first"] and j == 0
            last = it["last"] and j == nch - 1
            self.mm(bO, self.ps[bO][:, 0:qn], self.v2[:, c, vc * 128:(vc + 1) * 128], pt[:, psl], first, last,
                    [("v2", c), ptr])
            self.mm(bD, self.ps[bD][:, 0:qn], self.ones[:], pt[:, psl], first, last, [ptr, "ones"])
        if it["last"]:
            rr, ra_ = self.ft()
            if it["gi"] == 0:
                self.act(ra_[:, 0:qn], self.ps[bD][:, 0:qn], AF.Ln, [("ps", bD)], [rr])
                self.act(ra_[:, 0:qn], ra_[:, 0:qn], AF.Exp, [rr], [rr], scale=-1.0)
            else:
                self.recip(ra_[:, 0:qn], self.ps[bD][:, 0:qn], [("ps", bD)], [rr])
            self.tt(self.Ob[ob][:, i, qsl], self.ps[bO][:, 0:qn], ra_[:, 0:qn], ALU.mult, [("ps", bO), rr], [("O2", ob, i, t)])

    def wout_group(self, l, row0):
        wov = self.w_out[l].rearrange("(c p) m -> p c m", p=128)
        tiles = [self.wload(wov[:, row0 // 128:row0 // 128 + 4, hf * 1024:(hf + 1) * 1024], [128, 4, 1024]) for hf in range(2)]
        for oc in range(KC):
            sw, wv = tiles[oc // 8]
            col = (oc % 8) * 128
            for t in range(NTT):
                tsl = slice(t * TT, (t + 1) * TT)
                b = self.bank()
                for c in range(4):
                    self.mm(b, self.ps[b][:], wv[:, c, col:col + 128], self.Ob[c // 2][:, c % 2, tsl], c == 0, c == 3,
                            [("ws", sw), ("O2", c // 2, c % 2, t)])
                self.stt(self.x[:, oc, tsl], self.ps[b][:], self.gh[:, 1, oc:oc + 1], self.x[:, oc, tsl],
                         ALU.mult, ALU.add, [("ps", b), ("gh", 1), ("x", oc, t)], [("x", oc, t)])

    def projA(self, sw, wv, col0, m, rhs_fn, nk, t_list, n=TT):
        out = []
        for t in t_list:
            b = self.bank()
            for kc in range(nk):
                rap, rres = rhs_fn(kc, t)
                self.mm(b, self.ps[b][0:m, 0:n], wv[:, kc, col0:col0 + m], rap, kc == 0, kc == nk - 1, [("ws", sw), rres])
            out.append((t, b))
        return out

    def hrhs(self, kc, t):
        return self.h[:, kc, t * TT:(t + 1) * TT], ("h", kc, t)

    def vproj(self, l, gi, sw, wv, c0key, out_dram, ocol0):
        for c in range(8):
            t = c // 4
            b = self.bank()
            for kc in range(KC):
                self.mm(b, self.ps[b][:, 0:256], self.h[:, kc, c * 128:(c + 1) * 128], wv[:, kc, 0:256], kc == 0, kc == KC - 1,
                        [("ws", sw), ("h", kc, t)])
            if gi == 0:
                fr, fa = self.ft()
                self.copy(fa[:, 0:256], self.ps[b][:, 0:256], [("ps", b)], [fr], eng="dve")
                self.copy(self.v2[:, c0key + c, :], fa[:, 0:256], [fr], [("v2", c0key + c)])
                self.dma("sp", out_dram[l][c * 128:(c + 1) * 128, ocol0:ocol0 + 256], fa[:, 0:256], reads=[fr])
            else:
                self.copy(self.v2[:, c0key + c, :], self.ps[b][:, 0:256], [("ps", b)], [("v2", c0key + c)])

    def mixer(self, l, gi):
        self.join(self.ffn_res() + ["ez"], self.mix_res())
        self.ovl_owner = "mix"
        self.ring_i = 0
        self.prep_mods(l, gi, 1)
        self.P.add("pool", lambda e: e.memset(self.kr[64:128, :], 0.0), writes=[("kr", kt) for kt in range(3)])
        self.P.add("pool", lambda e: e.memset(self.qr2[64:128, :, :], 0.0),
                   writes=[("qr2", i, t) for i in range(2) for t in range(NTT)])
        self.modnorm(l, 1, gi)
        vb = l * NVL
        nkt = 2 if gi == 0 else 3
        koff = 0 if gi == 0 else PAST
        kc0 = koff // 128
        winv = self.w_in[l].rearrange("(kc p) m -> p kc m", p=128)
        if gi == 1:
            self.dma("pool", self.ckv[:, :, 0:PAST], self.c_ckvT[l], writes=[("ckv", 0, 0), ("ckv", 1, 0)])
            self.dma("pool", self.kr[0:64, 0:PAST], self.c_krT[l], writes=[("kr", 0)])
        wq = [self.wload(winv[:, :, j * 256:(j + 1) * 256], [128, KC, 256]) for j in range(2)]
        for t in range(NTT):
            tsl = slice(t * TT, (t + 1) * TT)
            raws = []
            for oc in range(4):
                sw, wv = wq[oc // 2]
                (_, b), = self.projA(sw, wv, (oc % 2) * 128, 128, self.hrhs, KC, [t])
                fr, fa = self.ft()
                self.copy(fa, self.ps[b][:], [("ps", b)], [fr], eng="dve")
                raws.append((fr, fa))
            self.rstd_bcast([(fa, [fr]) for fr, fa in raws], 512, self.rstd[:], "rstd")
            for oc, (fr, fa) in enumerate(raws):
                self.stt(self.cqn[:, oc, tsl], fa, self.vec[:, vb + VB_QN + oc:vb + VB_QN + oc + 1], self.rstd[:],
                         ALU.mult, ALU.mult, [fr, "rstd", "vec"], [("cqn", oc, t)])
        skv, wkv = self.wload(winv[:, :, S1:S2], [128, KC, 256])
        skr, wkr = self.wload(winv[:, :, S2:S3], [128, KC, 64])
        for t in range(NTT):
            tsl = slice(t * TT, (t + 1) * TT)
            ksl = slice(koff + t * TT, koff + (t + 1) * TT)
            kt = (koff + t * TT) // TT
            raws = []
            for oc in range(2):
                (_, b), = self.projA(skv, wkv, oc * 128, 128, self.hrhs, KC, [t])
                fr, fa = self.ft()
                self.copy(fa, self.ps[b][:], [("ps", b)], [fr], eng="dve")
                raws.append((fr, fa))
            self.rstd_bcast([(fa, [fr]) for fr, fa in raws], 256, self.rstd[:], "rstd")
            for oc, (fr, fa) in enumerate(raws):
                self.stt(fa, fa, self.vec[:, vb + VB_KVN + oc:vb + VB_KVN + oc + 1], self.rstd[:],
                         ALU.mult, ALU.mult, [fr, "rstd", "vec"], [fr])
                self.copy(self.ckv[:, oc, ksl], fa, [fr], [("ckv", oc, kt)])
                if gi == 0:
                    self.dma("sp", self.o_ckvT[l][:, oc, tsl], fa, reads=[fr])
            (_, b), = self.projA(skr, wkr, 0, 64, self.hrhs, KC, [t])
            fr, fa = self.ft()
            self.copy(fa[0:64, :], self.ps[b][0:64, :], [("ps", b)], [fr], eng="dve")
            if gi == 0:
                self.copy(self.kr[0:64, ksl], fa[0:64, :], [fr], [("kr", kt)])
                self.dma("sp", self.o_krT[l][:, tsl], fa[0:64, :], reads=[fr])
            else:
                self.rope(fa[0:64, :], fr, 64, t, self.kr[0:64, ksl], [("kr", kt)])
        if self.cfg.get("mix_stop", 9) <= 1:
            return
        self.prep_gate(l, gi, 1)
        wqbv = self.wqb[l].rearrange("(kc p) m -> p kc m", p=128)
        wkvbv = self.wkvb[l].rearrange("(kc p) m -> p kc m", p=128)
        sc_mla = 192.0 ** -0.5
        sc = 128.0 ** -0.5
        cq_rhs = lambda kc, t: (self.cqn[:, kc, t * TT:(t + 1) * TT], ("cqn", kc, t))
        ckv_rhs = lambda kc, kt: (self.ckv[:, kc, kt * TT:(kt + 1) * TT], ("ckv", kc, kt))
        for pr in range(4):
            sq_, wqp = self.wload(wqbv[:, :, pr * 384:(pr + 1) * 384], [128, 4, 384])
            sk_, wkp = self.wload(wkvbv[:, :, pr * 512:(pr + 1) * 512], [128, 2, 512])
            for i in range(2):
                for (t, b) in self.projA(sq_, wqp, i * 192, 128, cq_rhs, 4, range(NTT)):
                    self.copy(self.q2[:, i, t * TT:(t + 1) * TT], self.ps[b][:], [("ps", b)], [("q2", i, t)])
                for (t, b) in self.projA(sq_, wqp, i * 192 + 128, 64, cq_rhs, 4, range(NTT)):
                    tsl = slice(t * TT, (t + 1) * TT)
                    if gi == 0:
                        self.copy(self.qr2[0:64, i, tsl], self.ps[b][0:64, :], [("ps", b)], [("qr2", i, t)])
                    else:
                        fr, fa = self.ft()
                        self.copy(fa[0:64, :], self.ps[b][0:64, :], [("ps", b)], [fr], eng="dve")
                        self.rope(fa[0:64, :], fr, 64, t, self.qr2[0:64, i, tsl], [("qr2", i, t)])
                for (kt, b) in self.projA(sk_, wkp, i * 256, 128, ckv_rhs, 2, range(nkt)):
                    self.copy(self.k2[:, i, kt * TT:(kt + 1) * TT], self.ps[b][:], [("ps", b)], [("k2", i, kt)], eng="dve")
            for c in range(nkt * 4):
                b = self.bank()
                for i in range(2):
                    for kc in range(2):
                        self.mm(b, self.ps[b][:, i * 128:(i + 1) * 128], self.ckv[:, kc, c * 128:(c + 1) * 128],
                                wkp[:, kc, i * 256 + 128:i * 256 + 256], kc == 0, kc == 1, [("ws", sk_), ("ckv", kc, c // 4)])
                self.copy(self.v2[:, c, :], self.ps[b][:, 0:256], [("ps", b)], [("v2", c)])
            if self.cfg.get("mla_stop", 9) <= 1:
                continue
            self.attention(gi, pr % 2, [(0, 0, 0, None), (1, 1, 1, None)], sc_mla, True)
            if pr % 2 == 1:
                self.wout_group(l, (pr - 1) * 256)
        if self.cfg.get("mix_stop", 9) <= 2:
            return
        if gi == 1:
            self.join([("cqn", oc, t) for oc in range(4) for t in range(NTT)], ["ez"])
            self.build_ez(l)
        for pr in range(2):
            sq_, wqp = self.wload(winv[:, :, S3 + pr * 256:S3 + (pr + 1) * 256], [128, KC, 256])
            sk_, wkp = self.wload(winv[:, :, S3 + 512 + pr * 256:S3 + 512 + (pr + 1) * 256], [128, KC, 256])
            sv_, wvp = self.wload(winv[:, :, S3 + 1024 + pr * 256:S3 + 1024 + (pr + 1) * 256], [128, KC, 256])
            if gi == 1:
                for i in range(2):
                    self.dma("pool", self.k2[:, i, 0:PAST], self.c_nakT[l, 2 * pr + i], writes=[("k2", i, 0)])
                self.dma("pool", self.v2[:, 0:4, :], self.c_nav[l].rearrange("(c p) f -> p c f", p=128)[:, :, pr * 256:(pr + 1) * 256],
                         writes=[("v2", c) for c in range(4)])
            for i in range(2):
                for (t, b) in self.projA(sq_, wqp, i * 128, 128, self.hrhs, KC, range(NTT)):
                    self.copy(self.q2[:, i, t * TT:(t + 1) * TT], self.ps[b][:], [("ps", b)], [("q2", i, t)])
                for (t, b) in self.projA(sk_, wkp, i * 128, 128, self.hrhs, KC, range(NTT)):
                    kt = (koff + t * TT) // TT
                    if gi == 0:
                        fr, fa = self.ft()
                        self.copy(fa, self.ps[b][:], [("ps", b)], [fr], eng="dve")
                        self.copy(self.k2[:, i, koff + t * TT:koff + (t + 1) * TT], fa, [fr], [("k2", i, kt)])
                        self.dma("sp", self.o_nakT[l, 2 * pr + i][:, t * TT:(t + 1) * TT], fa, reads=[fr])
                    else:
                        self.copy(self.k2[:, i, koff + t * TT:koff + (t + 1) * TT], self.ps[b][:], [("ps", b)], [("k2", i, kt)])
            if self.cfg.get("na_stop", 9) <= 1:
                continue
            self.vproj(l, gi, sv_, wvp, kc0, self.o_nav, pr * 256)
            if self.cfg.get("na_stop", 9) <= 2:
                continue
            self.attention(gi, pr % 2, [(i, i, i, (2 * pr + i) if gi == 1 else None) for i in range(2)], sc, False)
            if pr % 2 == 1:
                self.wout_group(l, 1024)
        if self.cfg.get("mix_stop", 9) <= 3:
            return
        sk_, wkp = self.wload(winv[:, :, S5:S5 + 256], [128, KC, 256])
        sv_, wvp = self.wload(winv[:, :, S5 + 256:S5 + 512], [128, KC, 256])
        if gi == 1:
            for i in range(2):
                self.dma("pool", self.k2[:, i, 0:PAST], self.c_gkT[l, i], writes=[("k2", i, 0)])
            self.dma("pool", self.v2[:, 0:4, :], self.c_gv[l].rearrange("(c p) f -> p c f", p=128), writes=[("v2", c) for c in range(4)])

        def normed(b, gcol, t, out_bf, out_res, out_dram):
            r2r, r2 = self.ft()
            self.rstd_bcast([(self.ps[b][:], [("ps", b)])], 128, r2, r2r)
            fr, fa = self.ft()
            self.stt(fa, self.ps[b][:], self.vec[:, gcol:gcol + 1], r2, ALU.mult, ALU.mult, [("ps", b), r2r, "vec"], [fr])
            if gi == 0:
                self.copy(out_bf, fa, [fr], out_res)
                if out_dram is not None:
                    self.dma("sp", out_dram, fa, reads=[fr])
            else:
                self.rope(fa, fr, 128, t, out_bf, out_res)

        for i in range(2):
            for (t, b) in self.projA(sk_, wkp, i * 128, 128, self.hrhs, KC, range(NTT)):
                kt = (koff + t * TT) // TT
                normed(b, vb + VB_GKN, t, self.k2[:, i, koff + t * TT:koff + (t + 1) * TT], [("k2", i, kt)],
                       self.o_gkT[l, i][:, t * TT:(t + 1) * TT] if gi == 0 else None)
        self.vproj(l, gi, sv_, wvp, kc0, self.o_gv, 0)
        for pr in range(2):
            sq_, wqp = self.wload(winv[:, :, S4 + pr * 256:S4 + (pr + 1) * 256], [128, KC, 256])
            for i in range(2):
                for (t, b) in self.projA(sq_, wqp, i * 128, 128, self.hrhs, KC, range(NTT)):
                    normed(b, vb + VB_GQN, t, self.q2[:, i, t * TT:(t + 1) * TT], [("q2", i, t)], None)
            self.attention(gi, pr % 2, [(i, pr, pr, None) for i in range(2)], sc, False)
            if pr % 2 == 1:
                self.wout_group(l, 1536)


def _fm(a):
    t, d = a.shape
    return np.ascontiguousarray(a.T.reshape(d // 128, 128, t).transpose(1, 0, 2))


def _fm_inv(a):
    p, kc, t = a.shape
    return np.ascontiguousarray(a.transpose(1, 0, 2).reshape(kc * p, t).T)


def _vcols(v):
    return np.asarray(v, np.float32).reshape(-1, 128).T


def _rope_consts():
    def tables(d):
        q = d // 4
        t = np.arange(NT)
        pos = np.stack([t // 64, t % 64], axis=0).astype(np.float32)
        inv = (np.float32(10000.0) ** (-np.arange(q, dtype=np.float32) / np.float32(q))).astype(np.float32)
        cos = np.zeros((d, NT), np.float32)
        sin = np.zeros((d, NT), np.float32)
        rt = np.zeros((d, d), np.float32)
        for p in range(d):
            blk, which, j = p // (2 * q), (p // q) % 2, p % q
            ang = (pos[blk] * inv[j]).astype(np.float32)
            cos[p] = np.cos(ang)
            sin[p] = np.sin(ang)
            if which == 0:
                rt[p + q, p] = -1.0
            else:
                rt[p - q, p] = 1.0
        return cos, sin, rt
    c128, s128, r128 = tables(128)
    c64, s64, r64 = tables(64)
    rot = np.zeros((128, 192), np.float32)
    rot[:, 0:128] = r128
    rot[0:64, 128:192] = r64
    w1 = np.zeros((31, 127), np.float32)
    for dc in range(31):
        w1[dc, dc + 48] = 1.0
    vc = np.zeros((64, 64), np.float32)
    for qc in range(64):
        cs = min(max(qc - 8, 0), 48)
        vc[cs:cs + 16, qc] = 1.0
    return {"cos128": c128, "sin128": s128, "cos64": c64, "sin64": s64, "rotT": rot, "w1": w1, "validc": vc,
            "id2": np.eye(2, dtype=np.float32)}


def prepare_inputs(inp, cfg, ncores=NCORES):
    f = lambda k: np.asarray(inp[k], dtype=np.float32)
    vec_l = []
    for l in range(L):
        vec_l += [_vcols(f("ada_b")[l]), _vcols(f("norm_g")[l].reshape(-1)), _vcols(f("mla_q_norm")[l]),
                  _vcols(f("mla_kv_norm")[l]), _vcols(f("gqa_q_norm")[l]), _vcols(f("gqa_k_norm")[l])]
    vec_l.append(_vcols(f("final_norm")))
    vecs = np.ascontiguousarray(np.concatenate(vec_l, axis=1))
    shared = {"vecs": vecs}
    for k in ("ada_w", "ffn_wg", "ffn_wu", "ffn_wd", "w_in", "mla_wqb", "mla_wkvb", "w_out"):
        shared[k] = f(k)
    shared.update(_rope_consts())
    rpb = f("na_rpb")
    shared["rpbT"] = np.ascontiguousarray(rpb[:, :, ::-1, :].transpose(3, 0, 1, 2).reshape(31, L, 60))
    xp, xs, c, cctx = f("x_prompt"), f("x_sample"), f("c"), f("c_ctx")
    ckv, krp = f("cache_mla_ckv"), f("cache_mla_krope")
    nak, nav, gk, gv = f("cache_na_k"), f("cache_na_v"), f("cache_gqa_k"), f("cache_gqa_v")
    in_maps = []
    for k in range(ncores):
        m = dict(shared)
        m["xT_p"] = _fm(xp[4 * k:4 * k + 4].reshape(NT, D))
        m["xT_s"] = _fm(xs[k])
        m["condT"] = np.ascontiguousarray(np.stack([_vcols(cctx), _vcols(c[k])], axis=-1))
        m["c_ckvT"] = np.ascontiguousarray(ckv[k].transpose(0, 2, 1).reshape(L, 2, 128, PAST).transpose(0, 2, 1, 3))
        m["c_krT"] = np.ascontiguousarray(krp[k].transpose(0, 2, 1))
        m["c_nakT"] = np.ascontiguousarray(nak[k].transpose(0, 2, 3, 1))
        m["c_nav"] = np.ascontiguousarray(nav[k].reshape(L, PAST, 512))
        m["c_gkT"] = np.ascontiguousarray(gk[k].transpose(0, 2, 3, 1))
        m["c_gv"] = np.ascontiguousarray(gv[k].reshape(L, PAST, 256))
        in_maps.append(m)
    return in_maps


def assemble(outs):
    n = len(outs)
    y_prompt = np.stack([_fm_inv(o["yT_p"]) for o in outs]).reshape(4 * n, 256, D)
    y_sample = np.stack([_fm_inv(o["yT_s"]) for o in outs])
    def per_seq(a):
        nn, l, t, fdim = a.shape
        return np.ascontiguousarray(a.reshape(nn, l, 4, 256, fdim).transpose(0, 2, 1, 3, 4).reshape(nn * 4, l, 256, fdim))
    ckv = np.stack([o["o_ckvT"].transpose(0, 2, 1, 3).reshape(L, 256, NT).transpose(0, 2, 1) for o in outs])
    kr = np.stack([o["o_krT"].transpose(0, 2, 1) for o in outs])
    nak = np.stack([o["o_nakT"].transpose(0, 3, 1, 2).reshape(L, NT, 512) for o in outs])
    nav = np.stack([o["o_nav"] for o in outs])
    gk = np.stack([o["o_gkT"].transpose(0, 3, 1, 2).reshape(L, NT, 256) for o in outs])
    gv = np.stack([o["o_gv"] for o in outs])
    return (y_prompt, y_sample, per_seq(ckv), per_seq(kr),
            per_seq(nak).reshape(4 * n, L, 256, 4, 128), per_seq(nav).reshape(4 * n, L, 256, 4, 128),
            per_seq(gk).reshape(4 * n, L, 256, 2, 128), per_seq(gv).reshape(4 * n, L, 256, 2, 128))


_CACHE = {}


def run(inputs, cfg, ncores=NCORES, trace=False):
    key = tuple(sorted(cfg.items()))
    if key not in _CACHE:
        _CACHE[key] = Builder(dict(cfg)).build()
    nc = _CACHE[key]
    in_maps = prepare_inputs(inputs, cfg, ncores)
    res = run_bass_kernel_spmd(nc, in_maps, core_ids=list(range(ncores)), trace=trace)
    if trace:
        print("exec_time_ns", res.exec_time_ns)
    return res.results


def kernel(**inputs):
    cfg = {"layers": 2, "ffn1": True, "mixer": True, "ffn2": True}
    outs = run(inputs, cfg)
    return tuple(np.ascontiguousarray(a, dtype=np.float32) for a in assemble(outs))
```

```python
import numpy as np
from contextlib import ExitStack
import concourse.bass as bass
import concourse.mybir as mybir
from concourse.bass_utils import run_bass_kernel_spmd

F32 = mybir.dt.float32
BF16 = mybir.dt.bfloat16
AF = mybir.ActivationFunctionType
ALU = mybir.AluOpType

NCORES = 8
D = 2048
KC = 16
DFF = 5632
NFC = 44
NT = 1024
TT = 512
NTT = NT // TT
L = 2
EPS = 1e-6
PAST = 512
NS = 4
NFP = 8
NBP = 8
OVN = 23040
SLOT = 4096
S1, S2, S3 = 512, 768, 832
S4 = S3 + 1536
S5 = S4 + 512
INC = S5 + 512
NVL = 200
VB_ADAB, VB_NG, VB_QN, VB_KVN, VB_GQN, VB_GKN = 0, 144, 192, 196, 198, 199


class Op:
    __slots__ = ("eng", "fn", "deps", "sig", "rank", "dma", "dsi", "dval")


class Prog:
    def __init__(self, ndsem=8):
        self.ops = []
        self.lastw = {}
        self.rd_eng = {}
        self.rd_dma = {}
        self.ndsem = ndsem
        self.dq_cnt = {"pool": [0] * ndsem, "sp": [0] * ndsem}
        self.dq_last = {"pool": [None] * ndsem, "sp": [None] * ndsem}
        self.dq_rr = {"pool": 0, "sp": 0}

    def add(self, eng, fn, reads=(), writes=(), dma=False):
        idx = len(self.ops)
        op = Op()
        op.eng, op.fn, op.sig, op.rank, op.dma = eng, fn, False, 0, dma
        deps = set()
        for r in reads:
            w = self.lastw.get(r)
            if w is not None:
                deps.add(w)
        for r in writes:
            w = self.lastw.get(r)
            if w is not None:
                deps.add(w)
            for ri in self.rd_eng.get(r, {}).values():
                deps.add(ri)
            for ri in self.rd_dma.get(r, ()):
                deps.add(ri)
        if dma:
            k = self.dq_rr[eng] % self.ndsem
            self.dq_rr[eng] += 1
            prev = self.dq_last[eng][k]
            if prev is not None:
                deps.add(prev)
            self.dq_cnt[eng][k] += 1
            self.dq_last[eng][k] = idx
            op.dsi = k
            op.dval = 16 * self.dq_cnt[eng][k]
        for r in reads:
            if dma:
                self.rd_dma.setdefault(r, []).append(idx)
            else:
                self.rd_eng.setdefault(r, {})[eng] = idx
        for r in writes:
            self.lastw[r] = idx
            self.rd_eng[r] = {}
            self.rd_dma[r] = []
        deps.discard(idx)
        op.deps = deps
        self.ops.append(op)
        return idx

    def emit(self, nc):
        ops = self.ops
        for op in ops:
            for d in op.deps:
                dop = ops[d]
                if not dop.dma and not (dop.eng == "pe" and op.eng == "pe"):
                    dop.sig = True
        cnt = {}
        for op in ops:
            if not op.dma and op.sig:
                cnt[op.eng] = cnt.get(op.eng, 0) + 1
                op.rank = cnt[op.eng]
        with ExitStack() as es:
            esem = {e: es.enter_context(nc.semaphore("s_" + e)) for e in ("pe", "act", "dve", "pool")}
            dsem = {q: [es.enter_context(nc.semaphore("d_%s%d" % (q, i))) for i in range(self.ndsem)]
                    for q in ("pool", "sp")}
            block = es.enter_context(nc.Block())

            def make(eng_name):
                def body(e):
                    seen = {}
                    for op in ops:
                        if op.eng != eng_name:
                            continue
                        need = {}
                        for d in op.deps:
                            dop = ops[d]
                            if dop.dma:
                                key = ("d", dop.eng, dop.dsi)
                                val = dop.dval
                            else:
                                if dop.eng == "pe" and eng_name == "pe":
                                    continue
                                key = ("e", dop.eng)
                                val = dop.rank
                            if val > need.get(key, 0):
                                need[key] = val
                        for key, val in need.items():
                            if seen.get(key, 0) >= val:
                                continue
                            seen[key] = val
                            s = dsem[key[1]][key[2]] if key[0] == "d" else esem[key[1]]
                            e.wait_ge(s, val)
                        ins = op.fn(e)
                        if op.dma:
                            ins.then_inc(dsem[eng_name][op.dsi], 16)
                        elif op.sig:
                            ins.then_inc(esem[eng_name], 1)
                    if eng_name in ("pool", "sp"):
                        for k in range(self.ndsem):
                            if self.dq_cnt[eng_name][k]:
                                e.wait_ge(dsem[eng_name][k], 16 * self.dq_cnt[eng_name][k])
                return body

            block.tensor(make("pe"))
            block.scalar(make("act"))
            block.vector(make("dve"))
            block.gpsimd(make("pool"))
            block.sync(make("sp"))


class Builder:
    def __init__(self, cfg):
        self.cfg = cfg
        self.nc = bass.Bass("TRN2", target_bir_lowering=False)
        self.P = Prog()
        self.es = ExitStack()
        self.bank_i = 0
        self.ring_i = 0
        self.sq_i = 0
        self.held = set()
        self.ovl_owner = "ffn"
        self.ft_i = 0
        self.bt_i = 0

    def dram_in(self, name, shape, dt=F32):
        return self.nc.dram_tensor(name, list(shape), dt, kind="ExternalInput").ap()

    def dram_out(self, name, shape, dt=F32):
        return self.nc.dram_tensor(name, list(shape), dt, kind="ExternalOutput").ap()

    def sb(self, name, shape, dt):
        return self.es.enter_context(self.nc.sbuf_tensor(name, list(shape), dt))

    def bank(self, hold=False):
        while True:
            b = self.bank_i % 8
            self.bank_i += 1
            if b not in self.held:
                break
        if hold:
            self.held.add(b)
        return b

    def unhold(self, b):
        self.held.discard(b)

    def ft(self):
        i = self.ft_i % NFP
        self.ft_i += 1
        return ("fb", i), self.fpool[:, i, :]

    def bt(self):
        i = self.bt_i % NBP
        self.bt_i += 1
        return ("bb", i), self.bpool[:, i, :]

    def join(self, reads, writes):
        self.P.add("dve", lambda e: e.memset(self.jscr[:], 0.0), reads=list(reads) + ["jscr"], writes=list(writes) + ["jscr"])

    def wload(self, src, shape):
        lst = self.ring_ffn if self.ovl_owner == "ffn" else self.ring_mix
        s = lst[self.ring_i % len(lst)]
        self.ring_i += 1
        n = int(np.prod(shape[1:]))
        assert n <= SLOT, shape
        flat = self.ws[s][:, 0:n]
        if len(shape) == 3:
            view = flat.rearrange("p (a b) -> p a b", a=shape[1])
        else:
            view = flat
        self.P.add("pool", lambda e, o=self.split256(view), i=self.split256(src): e.dma_start(out=o, in_=i),
                   writes=[("ws", s)], dma=True)
        return s, view

    def mm(self, b, out, lhsT, rhs, start, stop, reads):
        self.P.add("pe", lambda e, o=out, l=lhsT, r=rhs, s0=start, s1=stop:
                   e.matmul(o, l, r, start=s0, stop=s1), reads=reads, writes=[("ps", b)])

    def build(self):
        nc, P, cfg = self.nc, self.P, self.cfg
        nl = cfg.get("layers", L)
        self.xT = {g: self.dram_in("xT_" + g, [128, KC, NT]) for g in "ps"}
        self.condT = self.dram_in("condT", [128, KC, 2])
        self.vecs = self.dram_in("vecs", [128, L * NVL + 16])
        self.ada_w = self.dram_in("ada_w", [L, D, 9 * D])
        self.wg = self.dram_in("ffn_wg", [L, 2, D, DFF])
        self.wu = self.dram_in("ffn_wu", [L, 2, D, DFF])
        self.wd = self.dram_in("ffn_wd", [L, 2, DFF, D])
        self.yT = {g: self.dram_out("yT_" + g, [128, KC, NT]) for g in "ps"}
        self.x = self.sb("x", [128, KC, NT], F32)
        self.h = self.sb("h", [128, KC, NT], BF16)
        self.ws = [self.sb("ws%d" % i, [128, SLOT], BF16) for i in range(NS)]
        self.ps = [self.es.enter_context(nc.psum_tensor("ps%d" % i, [128, TT], F32)) for i in range(8)]
        self.vec = self.sb("vec", [128, L * NVL + 16], F32)
        self.cond = self.sb("cond", [128, KC, 2], F32)
        self.scond = self.sb("scond", [128, KC, 2], BF16)
        self.modt = [self.sb("modt%d" % l, [128, 144, 2], F32) for l in range(L)]
        self.gs = self.sb("gs", [128, 3, KC], F32)
        self.gh = self.sb("gh", [128, 3, KC], F32)
        self.ones = self.sb("ones", [128, 128], BF16)
        self.rstd = self.sb("rstd", [128, TT], F32)
        self.rstd2 = self.sb("rstd2", [128, TT], F32)
        self.pre_stats = False
        self.fpool = self.sb("fpool", [128, NFP, TT], F32)
        self.bpool = self.sb("bpool", [128, NBP, TT], BF16)
        self.jscr = self.sb("jscr", [128, 2], F32)
        self.epsb = self.sb("epsb", [128, 2], F32)
        self.ovl = self.sb("ovl", [128, OVN], BF16)
        self.abuf = self.ovl[:, 0:12 * NT].rearrange("p (c t) -> p c t", c=12)
        self.ws = self.ws + [self.ovl[:, 12288:12288 + SLOT], self.ovl[:, 16384:16384 + SLOT]]
        self.ring_ffn = [0, 1, 2, 3, 4, 5]
        self.ring_mix = [0, 1, 2, 3]
        self.mixer_setup()
        self.nl = nl
        self.mod_bank = self.bank(hold=True)
        self.mod_next = 0
        self.mod_total = nl * 72

        P.add("pool", lambda e: e.memset(self.ones[:], 1.0), writes=["ones"])
        P.add("pool", lambda e: e.memset(self.epsb[:], EPS), writes=["epsb"])
        P.add("sp", lambda e: e.dma_start(out=self.vec[:], in_=self.vecs), writes=["vec"], dma=True)
        P.add("sp", lambda e: e.dma_start(out=self.cond[:], in_=self.condT), writes=["cond"], dma=True)
        P.add("act", lambda e: e.activation(out=self.scond[:], in_=self.cond[:], func=AF.Silu),
              reads=["cond"], writes=["scond"])
        for gi, g in enumerate("ps"):
            if g not in cfg.get("groups", "ps"):
                continue
            self.load_x(g)
            for l in range(nl):
                if cfg.get("ffn1", True):
                    self.ffn(l, 0, gi)
                if cfg.get("mixer", False):
                    self.mixer(l, gi)
                if cfg.get("ffn2", True):
                    self.ffn(l, 1, gi)
            self.final_norm(g)
        if cfg.get("dbg"):
            for name, (buf, shape, dt, res) in self.dbg_bufs().items():
                o = self.dram_out("dbg_" + name, shape, dt)
                P.add("sp", lambda e, o=o, buf=buf: e.dma_start(out=o, in_=buf), reads=res, dma=True)
        P.emit(nc)
        return nc

    def dbg_bufs(self):
        return {
            "x": (self.x[:], [128, KC, NT], F32, [("x", kc, t) for kc in range(KC) for t in range(NTT)]),
            "h": (self.h[:], [128, KC, NT], BF16, [("h", kc, t) for kc in range(KC) for t in range(NTT)]),
            "ovl": (self.ovl[:], [128, OVN], BF16, self.mix_res() + ["ez"]),
        }

    def ffn_res(self):
        return [("a", j, t) for j in range(12) for t in range(NTT)] + [("ws", 4), ("ws", 5)]

    def act(self, out, in_, func, reads, writes, **kw):
        self.P.add("act", lambda e, o=out, i=in_, f=func, kw=kw: e.activation(out=o, in_=i, func=f, **kw),
                   reads=reads, writes=writes)

    def tt(self, out, in0, in1, op, reads, writes, eng="dve"):
        self.P.add(eng, lambda e, o=out, a=in0, b=in1, op=op: e.tensor_tensor(out=o, in0=a, in1=b, op=op),
                   reads=reads, writes=writes)

    def ts(self, out, in0, s1, s2, op0, op1, reads, writes, eng="dve"):
        if op1 is None:
            self.P.add(eng, lambda e, o=out, a=in0, s1=s1, op0=op0: e.tensor_scalar(
                out=o, in0=a, scalar1=s1, scalar2=None, op0=op0), reads=reads, writes=writes)
        else:
            self.P.add(eng, lambda e, o=out, a=in0, s1=s1, s2=s2, op0=op0, op1=op1: e.tensor_scalar(
                out=o, in0=a, scalar1=s1, scalar2=s2, op0=op0, op1=op1), reads=reads, writes=writes)

    def stt(self, out, in0, scalar, in1, op0, op1, reads, writes, eng="dve"):
        self.P.add(eng, lambda e, o=out, a=in0, s=scalar, b=in1, op0=op0, op1=op1: e.scalar_tensor_tensor(
            out=o, in0=a, scalar=s, in1=b, op0=op0, op1=op1), reads=reads, writes=writes)

    def recip(self, out, in_, reads, writes):
        self.P.add("dve", lambda e, o=out, i=in_: e.reciprocal(out=o, in_=i), reads=reads, writes=writes)

    def dma(self, q, out, in_, reads=(), writes=()):
        if q == "pool":
            out, in_ = self.split256(out), self.split256(in_)
        self.P.add(q, lambda e, o=out, i=in_: e.dma_start(out=o, in_=i), reads=reads, writes=writes, dma=True)

    @staticmethod
    def split256(ap):
        n = ap.shape[-1]
        if n <= 256 or n % 256:
            return ap
        nd = len(ap.shape)
        if nd == 2:
            return ap.rearrange("p (a b) -> p a b", b=256)
        if nd == 3:
            return ap.rearrange("p c (a b) -> p c a b", b=256)
        return ap

    def mod_job(self):
        j = self.mod_next
        self.mod_next += 1
        l, ti = divmod(j, 72)
        i = ti // 8
        b = self.mod_bank
        av = self.ada_w[l].rearrange("(kc p) m -> p kc m", p=128)
        s, wv = self.wload(av[:, :, ti * 256:(ti + 1) * 256], [128, KC, 256])
        for m in range(2):
            ocl = (i % 2) * 16 + (ti % 8) * 2 + m
            for kc in range(KC):
                self.mm(b, self.ps[b][:, ocl * 2:ocl * 2 + 2], wv[:, kc, m * 128:(m + 1) * 128], self.scond[:, kc, :],
                        kc == 0, kc == KC - 1, [("ws", s), "scond"])
        if ti % 8 == 7:
            base = l * NVL + VB_ADAB + i * 16
            c0 = (i % 2) * 32
            mps = self.ps[b][:, c0:c0 + 32].rearrange("p (c s) -> p c s", s=2)
            for s_ in range(2):
                self.tt(self.modt[l][:, i * 16:(i + 1) * 16, s_], mps[:, :, s_], self.vec[:, base:base + 16], ALU.add,
                        [("ps", b), "vec"], [("modt", l, i)])
        if self.mod_next == self.mod_total:
            self.unhold(self.mod_bank)

    def mod_need(self, l, i3):
        target = min(l * 72 + (3 * i3 + 3) * 8, self.mod_total)
        while self.mod_next < target:
            self.mod_job()

    def bg(self, n=1):
        for _ in range(n):
            if self.mod_next < self.mod_total:
                self.mod_job()

    def load_x(self, g):
        for kc4 in range(0, KC, 4):
            self.dma("sp", self.x[:, kc4:kc4 + 4, :], self.xT[g][:, kc4:kc4 + 4, :],
                     writes=[("x", kc, t) for kc in range(kc4, kc4 + 4) for t in range(NTT)])

    def prep_mods(self, l, gi, i):
        target = min(l * 72 + (3 * i + 2) * 8, self.mod_total)
        while self.mod_next < target:
            self.mod_job()
        m = self.modt[l]
        sc = m[:, (3 * i + 1) * 16:(3 * i + 2) * 16, gi]
        ng = self.vec[:, l * NVL + VB_NG + i * 16: l * NVL + VB_NG + (i + 1) * 16]
        self.stt(self.gs[:, i, :], sc, 1.0, ng, ALU.add, ALU.mult, [("modt", l, 3 * i + 1), "vec"], [("gs", i)])

    def prep_gate(self, l, gi, i):
        self.mod_need(l, i)
        m = self.modt[l]
        gt = m[:, (3 * i + 2) * 16:(3 * i + 3) * 16, gi]
        self.ts(self.gh[:, i, :], gt, (1.0 if i == 1 else 0.5), None, ALU.mult, None, [("modt", l, 3 * i + 2)], [("gh", i)])

    def rstd_bcast(self, srcs, nfeat, rstd, rres):
        b = self.bank()
        n = srcs[0][0].shape[-1]
        for k, (ap, rd) in enumerate(srcs):
            qr, qa = self.bt()
            self.act(qa[:, 0:n], ap, AF.Square, rd, [qr])
            self.mm(b, self.ps[b][:, 0:n], self.ones[:], qa[:, 0:n], k == 0, k == len(srcs) - 1,
                    [qr, "ones"])
        self.act(rstd, self.ps[b][:, 0:n], AF.Ln, [("ps", b), "epsb"], [rres], scale=1.0 / nfeat, bias=self.epsb[:, 0:1])
        self.act(rstd, rstd, AF.Exp, [rres], [rres], scale=-0.5)

    def stats_open(self):
        self.st = [self.bank(hold=True), self.bank(hold=True)]
        self.st_n = 0

    def stats_chunk(self, kc):
        for t in range(NTT):
            tsl = slice(t * TT, (t + 1) * TT)
            qr, qa = self.bt()
            self.act(qa, self.x[:, kc, tsl], AF.Square, [("x", kc, t)], [qr])
            b = self.st[t]
            self.mm(b, self.ps[b][:], self.ones[:], qa, self.st_n == 0, self.st_n == KC - 1, [qr, "ones"])
        self.st_n += 1

    def stats_close(self):
        assert self.st_n == KC
        for t, (rt, rres) in enumerate(((self.rstd, "rstd"), (self.rstd2, "rstd2"))):
            b = self.st[t]
            self.act(rt[:], self.ps[b][:], AF.Ln, [("ps", b), "epsb"], [rres], scale=1.0 / D, bias=self.epsb[:, 0:1])
            self.act(rt[:], rt[:], AF.Exp, [rres], [rres], scale=-0.5)
            self.unhold(b)
        self.pre_stats = True

    def norm_rstd(self, t):
        if self.pre_stats:
            return ((self.rstd, "rstd"), (self.rstd2, "rstd2"))[t]
        self.rstd_from_x(t)
        return (self.rstd, "rstd")

    def rstd_from_x(self, t):
        tsl = slice(t * TT, (t + 1) * TT)
        self.rstd_bcast([(self.x[:, kc, tsl], [("x", kc, t)]) for kc in range(KC)], D, self.rstd[:], "rstd")

    def modnorm(self, l, i, gi):
        m = self.modt[l]
        for t in range(NTT):
            tsl = slice(t * TT, (t + 1) * TT)
            rt, rres = self.norm_rstd(t)
            for kc in range(KC):
                nr, na = self.ft()
                self.tt(na, self.x[:, kc, tsl], rt[:], ALU.mult,
                        [("x", kc, t), rres], [nr])
                sh = m[:, 3 * i * 16 + kc, gi:gi + 1]
                self.act(self.h[:, kc, tsl], na, AF.Identity,
                         [nr, ("gs", i), ("modt", l, 3 * i)], [("h", kc, t)],
                         scale=self.gs[:, i, kc:kc + 1], bias=sh)
        self.pre_stats = False

    def ffn(self, l, fi, gi):
        i = 0 if fi == 0 else 2
        if self.ovl_owner != "ffn":
            self.join(self.mix_res() + ["ez"], self.ffn_res())
            self.ovl_owner = "ffn"
            self.ring_i = 0
        self.prep_mods(l, gi, i)
        self.modnorm(l, i, gi)
        wgv = self.wg[l, fi].rearrange("(kc p) m -> p kc m", p=128)
        wuv = self.wu[l, fi].rearrange("(kc p) m -> p kc m", p=128)
        wdv = self.wd[l, fi].rearrange("(c p) m -> p c m", p=128)
        segs = [(0, 12), (12, 12), (24, 10), (34, 10)]
        for (c0, n) in segs:
            for jt in range(n // 2):
                col = (c0 + 2 * jt) * 128
                sg_, gv = self.wload(wgv[:, :, col:col + 256], [128, KC, 256])
                su_, uv = self.wload(wuv[:, :, col:col + 256], [128, KC, 256])
                for m in range(2):
                    jl = 2 * jt + m
                    for t in range(NTT):
                        tsl = slice(t * TT, (t + 1) * TT)
                        bg = self.bank()
                        bu = self.bank()
                        for kc in range(KC):
                            self.mm(bg, self.ps[bg][:], gv[:, kc, m * 128:(m + 1) * 128], self.h[:, kc, tsl],
                                    kc == 0, kc == KC - 1, [("ws", sg_), ("h", kc, t)])
                        for kc in range(KC):
                            self.mm(bu, self.ps[bu][:], uv[:, kc, m * 128:(m + 1) * 128], self.h[:, kc, tsl],
                                    kc == 0, kc == KC - 1, [("ws", su_), ("h", kc, t)])
                        sr, sa = self.ft()
                        self.act(sa, self.ps[bg][:], AF.Silu, [("ps", bg)], [sr])
                        self.tt(self.abuf[:, jl, tsl], self.ps[bu][:], sa, ALU.mult,
                                [("ps", bu), sr], [("a", jl, t)])
                self.bg(1)
            if c0 == 0:
                self.prep_gate(l, gi, i)
            last_seg = (c0 == segs[-1][0])
            if last_seg:
                self.stats_open()
                prev_oc = None
            for ot in range(8):
                sd_, dv = self.wload(wdv[:, c0:c0 + n, ot * 256:(ot + 1) * 256], [128, n, 256])
                for m in range(2):
                    oc = 2 * ot + m
                    for t in range(NTT):
                        tsl = slice(t * TT, (t + 1) * TT)
                        b = self.bank()
                        for c in range(n):
                            self.mm(b, self.ps[b][:], dv[:, c, m * 128:(m + 1) * 128], self.abuf[:, c, tsl],
                                    c == 0, c == n - 1, [("ws", sd_), ("a", c, t)])
                        self.stt(self.x[:, oc, tsl], self.ps[b][:], self.gh[:, i, oc:oc + 1], self.x[:, oc, tsl],
                                 ALU.mult, ALU.add, [("ps", b), ("gh", i), ("x", oc, t)], [("x", oc, t)])
                if last_seg:
                    if prev_oc is not None:
                        for kc_ in prev_oc:
                            self.stats_chunk(kc_)
                    prev_oc = (2 * ot, 2 * ot + 1)
                self.bg(1)
            if last_seg:
                for kc_ in prev_oc:
                    self.stats_chunk(kc_)
                self.stats_close()

    def final_norm(self, g):
        fb = L * NVL
        for t in range(NTT):
            tsl = slice(t * TT, (t + 1) * TT)
            rt, rres = self.norm_rstd(t)
            for kc in range(KC):
                orr, oa = self.ft()
                self.stt(oa, self.x[:, kc, tsl], self.vec[:, fb + kc:fb + kc + 1], rt[:],
                         ALU.mult, ALU.mult, [("x", kc, t), rres, "vec"], [orr])
                self.dma("sp", self.yT[g][:, kc, tsl], oa, reads=[orr])
        self.pre_stats = False

    def mixer_setup(self):
        nc = self.nc
        o = self.ovl
        def v3(off, a, b):
            return o[:, off:off + a * b].rearrange("p (a b) -> p a b", a=a)
        self.q2 = v3(0, 2, NT)
        self.k2 = v3(2048, 2, 1536)
        self.v2 = v3(5120, 12, 256)
        self.O2 = v3(8192, 2, NT)
        self.Ob = [self.O2, v3(20992, 2, NT)]
        assert 20992 + 2 * NT <= OVN
        self.qr2 = v3(10240, 2, NT)
        self.cqn = v3(12288, 4, NT)
        self.ez = o[:, 12288:12288 + 4096].rearrange("p (h u q) -> p h u q", h=4, u=16)
        self.ckv = v3(16384, 2, 1536)
        self.kr = o[:, 19456:19456 + 1536]
        assert 19456 + 1536 <= OVN
        self.w_in = self.dram_in("w_in", [L, D, INC])
        self.wqb = self.dram_in("mla_wqb", [L, 512, 1536])
        self.wkvb = self.dram_in("mla_wkvb", [L, 256, 2048])
        self.w_out = self.dram_in("w_out", [L, D, D])
        self.c_ckvT = self.dram_in("c_ckvT", [L, 128, 2, PAST])
        self.c_krT = self.dram_in("c_krT", [L, 64, PAST])
        self.c_nakT = self.dram_in("c_nakT", [L, 4, 128, PAST])
        self.c_nav = self.dram_in("c_nav", [L, PAST, 512])
        self.c_gkT = self.dram_in("c_gkT", [L, 2, 128, PAST])
        self.c_gv = self.dram_in("c_gv", [L, PAST, 256])
        self.d_cos128 = self.dram_in("cos128", [128, NT])
        self.d_sin128 = self.dram_in("sin128", [128, NT])
        self.d_cos64 = self.dram_in("cos64", [64, NT])
        self.d_sin64 = self.dram_in("sin64", [64, NT])
        self.d_rot = self.dram_in("rotT", [128, 192])
        self.d_w1 = self.dram_in("w1", [31, 127])
        self.d_vc = self.dram_in("validc", [64, 64])
        self.d_rpbT = self.dram_in("rpbT", [31, L, 60])
        self.o_ckvT = self.dram_out("o_ckvT", [L, 128, 2, NT])
        self.o_krT = self.dram_out("o_krT", [L, 64, NT])
        self.o_nakT = self.dram_out("o_nakT", [L, 4, 128, NT])
        self.o_nav = self.dram_out("o_nav", [L, NT, 512])
        self.o_gkT = self.dram_out("o_gkT", [L, 2, 128, NT])
        self.o_gv = self.dram_out("o_gv", [L, NT, 256])
        self.rot = self.sb("rot", [128, 192], F32)
        self.w1 = self.sb("w1s", [31, 127], F32)
        self.vc = self.sb("vcs", [64, 64], BF16)
        self.vcf = self.sb("vcf", [64, 64], F32)
        self.rpb = self.sb("rpbs", [31, L, 60], F32)
        self.dma("sp", self.rot[:], self.d_rot, writes=["rot"])
        self.dma("sp", self.w1[:], self.d_w1, writes=["nac"])
        self.dma("sp", self.vcf[:], self.d_vc, writes=["vcf"])
        self.dma("sp", self.rpb[:], self.d_rpbT, writes=["nac2"])
        self.P.add("dve", lambda e: e.tensor_copy(self.vc[:], self.vcf[:]), reads=["vcf"], writes=["vc"])

    def mix_res(self):
        r = []
        for i in range(2):
            for t in range(NTT):
                r += [("q2", i, t), ("qr2", i, t), ("O2", 0, i, t), ("O2", 1, i, t)]
            for kt in range(3):
                r += [("k2", i, kt), ("ckv", i, kt)]
        r += [("kr", kt) for kt in range(3)]
        r += [("v2", c) for c in range(12)]
        r += [("cqn", oc, t) for oc in range(4) for t in range(NTT)]
        return r

    def copy(self, out, in_, reads, writes, eng="act"):
        if eng == "act":
            self.P.add("act", lambda e, o=out, i=in_: e.copy(out=o, in_=i), reads=reads, writes=writes)
        else:
            self.P.add(eng, lambda e, o=out, i=in_: e.tensor_copy(o, i), reads=reads, writes=writes)

    def rope(self, a_ap, a_res, npart, t, out_ap, out_res):
        tsl = slice(t * TT, (t + 1) * TT)
        dc, ds = (self.d_cos128, self.d_sin128) if npart == 128 else (self.d_cos64, self.d_sin64)
        rt = self.rot[:, 0:128] if npart == 128 else self.rot[0:64, 128:192]
        cr, ca = self.ft()
        self.dma("sp", ca[0:npart, :], dc[:, tsl], writes=[cr])
        sr, sa = self.ft()
        self.dma("sp", sa[0:npart, :], ds[:, tsl], writes=[sr])
        b = self.bank()
        self.mm(b, self.ps[b][0:npart, :], rt, a_ap, True, True, [a_res, "rot"])
        t1r, t1 = self.ft()
        self.tt(t1[0:npart, :], a_ap, ca[0:npart, :], ALU.mult, [a_res, cr], [t1r])
        t2r, t2 = self.ft()
        self.tt(t2[0:npart, :], self.ps[b][0:npart, :], sa[0:npart, :], ALU.mult, [("ps", b), sr], [t2r])
        self.tt(out_ap, t1[0:npart, :], t2[0:npart, :], ALU.add, [t1r, t2r], out_res)

    def build_ez(self, l):
        for q8 in range(8):
            b = self.bank()
            for qcl in range(8):
                qc = q8 * 8 + qcl
                self.mm(b, self.ps[b][0:64, qcl * 60:(qcl + 1) * 60], self.w1[:, 63 - qc:127 - qc], self.rpb[:, l, :],
                        True, True, ["nac", "nac2"])
            pin = self.ps[b][0:64, 0:480].rearrange("p (q h u) -> p h u q", q=8, h=4)
            self.act(self.ez[0:64, :, 0:15, q8 * 8:(q8 + 1) * 8], pin, AF.Exp, [("ps", b)], ["ez"])
        for hh in range(4):
            for u in range(15):
                self.tt(self.ez[0:64, hh, u, :], self.ez[0:64, hh, u, :], self.vc[:], ALU.mult, ["ez", "vc"], ["ez"])
        self.dma("sp", self.ez[64:128, :, 1:16, :], self.ez[0:64, :, 0:15, :], reads=["ez"], writes=["ez"])

    def attention(self, gi, ob, heads, scale, mla, la=2):
        items = []
        for (i, ki, vc, na_head) in heads:
            base = dict(i=i, ki=ki, vc=vc, na_head=na_head)
            if gi == 0:
                for s_ in range(4):
                    items.append(dict(base, q0=256 * s_, qn=256, chunks=[(2 * s_, None), (2 * s_ + 1, None)], first=True, last=True))
            else:
                for qt in range(2):
                    if na_head is None:
                        ch = [(c, None) for c in range(12)]
                    else:
                        r0 = 8 * qt
                        ch = [(c, None) for c in range(4)]
                        for kr0 in (range(0, 12, 2) if qt == 0 else range(4, 16, 2)):
                            halves = []
                            for hf in range(2):
                                krow = kr0 + hf
                                rs = [r for r in range(r0, r0 + 8) if min(max(r - 4, 0), 8) <= krow <= min(max(r - 4, 0), 8) + 7]
                                halves.append((rs[0], rs[-1] + 1) if rs else None)
                            ch.append((4 + kr0 // 2, (kr0, r0, halves)))
                    for idx, cn in enumerate(ch):
                        items.append(dict(base, q0=512 * qt, qn=512, chunks=[cn], first=idx == 0, last=idx == len(ch) - 1))
        qseq = -1
        for it in items:
            if it["first"]:
                qseq += 1
            it["qseq"] = qseq
            it["gi"] = gi
        self.att_pairs = [(self.bank(hold=True), self.bank(hold=True)) for _ in range(2)]
        pend = []
        for it in items:
            self.att_S(it, scale, mla)
            pend.append(it)
            if len(pend) > la:
                self.att_PV(pend.pop(0), ob)
        for it in pend:
            self.att_PV(it, ob)
        for (b0, b1) in self.att_pairs:
            self.unhold(b0)
            self.unhold(b1)

    def att_S(self, it, scale, mla):
        i, ki, q0, qn = it["i"], it["ki"], it["q0"], it["qn"]
        t = q0 // TT
        qsl = slice(q0, q0 + qn)
        bS = self.bank()
        nch = len(it["chunks"])
        for j, (c, na) in enumerate(it["chunks"]):
            csl = slice(c * 128, (c + 1) * 128)
            kt = c // 4
            osl = slice(j * qn, (j + 1) * qn)
            self.mm(bS, self.ps[bS][:, osl], self.k2[:, ki, csl], self.q2[:, i, qsl], True, not mla,
                    [("k2", ki, kt), ("q2", i, t)])
            if mla:
                self.mm(bS, self.ps[bS][:, osl], self.kr[:, csl], self.qr2[:, i, qsl], False, True,
                        [("kr", kt), ("qr2", i, t)])
        ptr, pt = self.bt()
        (c, na) = it["chunks"][0]
        if na is None:
            self.act(pt[:, 0:nch * qn], self.ps[bS][:, 0:nch * qn], AF.Exp, [("ps", bS)], [ptr], scale=scale)
        else:
            (kr0, r0, halves) = na
            self.P.add("pool", lambda e, o=pt: e.memset(o, 0.0), writes=[ptr])
            tr, tm = self.bt()
            for hf in range(2):
                if halves[hf] is None:
                    continue
                ra, rb = halves[hf]
                psl = slice(64 * hf, 64 * hf + 64)
                fsl = slice((ra - r0) * 64, (rb - r0) * 64)
                self.act(tm[psl, fsl], self.ps[bS][psl, fsl], AF.Exp, [("ps", bS)], [tr], scale=scale)
                u0 = 7 - kr0
                ezv = self.ez[psl, it["na_head"], u0 + ra:u0 + rb, :]
                self.tt(pt[psl, fsl].rearrange("p (r q) -> p r q", q=64), tm[psl, fsl].rearrange("p (r q) -> p r q", q=64),
                        ezv, ALU.mult, [tr, "ez", ptr], [ptr])
        it["pt"] = (ptr, pt)

    def att_PV(self, it, ob):
        i, vc, q0, qn = it["i"], it["vc"], it["q0"], it["qn"]
        t = q0 // TT
        qsl = slice(q0, q0 + qn)
        bO, bD = self.att_pairs[it["qseq"] % 2]
        ptr, pt = it["pt"]
        nch = len(it["chunks"])
        for j, (c, na) in enumerate(it["chunks"]):
            psl = slice(j * qn, (j + 1) * qn)
            first = it["first"] and j == 0
            last = it["last"] and j == nch - 1
            self.mm(bO, self.ps[bO][:, 0:qn], self.v2[:, c, vc * 128:(vc + 1) * 128], pt[:, psl], first, last,
                    [("v2", c), ptr])
            self.mm(bD, self.ps[bD][:, 0:qn], self.ones[:], pt[:, psl], first, last, [ptr, "ones"])
        if it["last"]:
            rr, ra_ = self.ft()
            if it["gi"] == 0:
                self.act(ra_[:, 0:qn], self.ps[bD][:, 0:qn], AF.Ln, [("ps", bD)], [rr])
                self.act(ra_[:, 0:qn], ra_[:, 0:qn], AF.Exp, [rr], [rr], scale=-1.0)
            else:
                self.recip(ra_[:, 0:qn], self.ps[bD][:, 0:qn], [("ps", bD)], [rr])
            self.tt(self.Ob[ob][:, i, qsl], self.ps[bO][:, 0:qn], ra_[:, 0:qn], ALU.mult, [("ps", bO), rr], [("O2", ob, i, t)])

    def wout_group(self, l, row0, last=False):
        wov = self.w_out[l].rearrange("(c p) m -> p c m", p=128)
        tiles = [self.wload(wov[:, row0 // 128:row0 // 128 + 4, hf * 1024:(hf + 1) * 1024], [128, 4, 1024]) for hf in range(2)]
        if last:
            self.stats_open()
        for oc in range(KC):
            if last and oc > 0:
                pass
            sw, wv = tiles[oc // 8]
            col = (oc % 8) * 128
            for t in range(NTT):
                tsl = slice(t * TT, (t + 1) * TT)
                b = self.bank()
                for c in range(4):
                    self.mm(b, self.ps[b][:], wv[:, c, col:col + 128], self.Ob[c // 2][:, c % 2, tsl], c == 0, c == 3,
                            [("ws", sw), ("O2", c // 2, c % 2, t)])
                self.stt(self.x[:, oc, tsl], self.ps[b][:], self.gh[:, 1, oc:oc + 1], self.x[:, oc, tsl],
                         ALU.mult, ALU.add, [("ps", b), ("gh", 1), ("x", oc, t)], [("x", oc, t)])
            if last and oc > 0:
                self.stats_chunk(oc - 1)
        if last:
            self.stats_chunk(KC - 1)
            self.stats_close()

    def projA(self, sw, wv, col0, m, rhs_fn, nk, t_list, n=TT):
        out = []
        for t in t_list:
            b = self.bank()
            for kc in range(nk):
                rap, rres = rhs_fn(kc, t)
                self.mm(b, self.ps[b][0:m, 0:n], wv[:, kc, col0:col0 + m], rap, kc == 0, kc == nk - 1, [("ws", sw), rres])
            out.append((t, b))
        return out

    def hrhs(self, kc, t):
        return self.h[:, kc, t * TT:(t + 1) * TT], ("h", kc, t)

    def vproj(self, l, gi, sw, wv, c0key, out_dram, ocol0):
        for c in range(8):
            t = c // 4
            b = self.bank()
            for kc in range(KC):
                self.mm(b, self.ps[b][:, 0:256], self.h[:, kc, c * 128:(c + 1) * 128], wv[:, kc, 0:256], kc == 0, kc == KC - 1,
                        [("ws", sw), ("h", kc, t)])
            if gi == 0:
                fr, fa = self.ft()
                self.copy(fa[:, 0:256], self.ps[b][:, 0:256], [("ps", b)], [fr], eng="dve")
                self.copy(self.v2[:, c0key + c, :], fa[:, 0:256], [fr], [("v2", c0key + c)])
                self.dma("sp", out_dram[l][c * 128:(c + 1) * 128, ocol0:ocol0 + 256], fa[:, 0:256], reads=[fr])
            else:
                self.copy(self.v2[:, c0key + c, :], self.ps[b][:, 0:256], [("ps", b)], [("v2", c0key + c)])

    def mixer(self, l, gi):
        self.join(self.ffn_res() + ["ez"], self.mix_res())
        self.ovl_owner = "mix"
        self.ring_i = 0
        self.prep_mods(l, gi, 1)
        self.P.add("pool", lambda e: e.memset(self.kr[64:128, :], 0.0), writes=[("kr", kt) for kt in range(3)])
        self.P.add("pool", lambda e: e.memset(self.qr2[64:128, :, :], 0.0),
                   writes=[("qr2", i, t) for i in range(2) for t in range(NTT)])
        self.modnorm(l, 1, gi)
        vb = l * NVL
        nkt = 2 if gi == 0 else 3
        koff = 0 if gi == 0 else PAST
        kc0 = koff // 128
        winv = self.w_in[l].rearrange("(kc p) m -> p kc m", p=128)
        if gi == 1:
            self.dma("pool", self.ckv[:, :, 0:PAST], self.c_ckvT[l], writes=[("ckv", 0, 0), ("ckv", 1, 0)])
            self.dma("pool", self.kr[0:64, 0:PAST], self.c_krT[l], writes=[("kr", 0)])
        wq = [self.wload(winv[:, :, j * 256:(j + 1) * 256], [128, KC, 256]) for j in range(2)]
        for t in range(NTT):
            tsl = slice(t * TT, (t + 1) * TT)
            raws = []
            for oc in range(4):
                sw, wv = wq[oc // 2]
                (_, b), = self.projA(sw, wv, (oc % 2) * 128, 128, self.hrhs, KC, [t])
                fr, fa = self.ft()
                self.copy(fa, self.ps[b][:], [("ps", b)], [fr], eng="dve")
                raws.append((fr, fa))
            self.rstd_bcast([(fa, [fr]) for fr, fa in raws], 512, self.rstd[:], "rstd")
            for oc, (fr, fa) in enumerate(raws):
                self.stt(self.cqn[:, oc, tsl], fa, self.vec[:, vb + VB_QN + oc:vb + VB_QN + oc + 1], self.rstd[:],
                         ALU.mult, ALU.mult, [fr, "rstd", "vec"], [("cqn", oc, t)])
        skv, wkv = self.wload(winv[:, :, S1:S2], [128, KC, 256])
        skr, wkr = self.wload(winv[:, :, S2:S3], [128, KC, 64])
        for t in range(NTT):
            tsl = slice(t * TT, (t + 1) * TT)
            ksl = slice(koff + t * TT, koff + (t + 1) * TT)
            kt = (koff + t * TT) // TT
            raws = []
            for oc in range(2):
                (_, b), = self.projA(skv, wkv, oc * 128, 128, self.hrhs, KC, [t])
                fr, fa = self.ft()
                self.copy(fa, self.ps[b][:], [("ps", b)], [fr], eng="dve")
                raws.append((fr, fa))
            self.rstd_bcast([(fa, [fr]) for fr, fa in raws], 256, self.rstd[:], "rstd")
            for oc, (fr, fa) in enumerate(raws):
                self.stt(fa, fa, self.vec[:, vb + VB_KVN + oc:vb + VB_KVN + oc + 1], self.rstd[:],
                         ALU.mult, ALU.mult, [fr, "rstd", "vec"], [fr])
                self.copy(self.ckv[:, oc, ksl], fa, [fr], [("ckv", oc, kt)])
                if gi == 0:
                    self.dma("sp", self.o_ckvT[l][:, oc, tsl], fa, reads=[fr])
            (_, b), = self.projA(skr, wkr, 0, 64, self.hrhs, KC, [t])
            fr, fa = self.ft()
            self.copy(fa[0:64, :], self.ps[b][0:64, :], [("ps", b)], [fr], eng="dve")
            if gi == 0:
                self.copy(self.kr[0:64, ksl], fa[0:64, :], [fr], [("kr", kt)])
                self.dma("sp", self.o_krT[l][:, tsl], fa[0:64, :], reads=[fr])
            else:
                self.rope(fa[0:64, :], fr, 64, t, self.kr[0:64, ksl], [("kr", kt)])
        if self.cfg.get("mix_stop", 9) <= 1:
            return
        self.prep_gate(l, gi, 1)
        wqbv = self.wqb[l].rearrange("(kc p) m -> p kc m", p=128)
        wkvbv = self.wkvb[l].rearrange("(kc p) m -> p kc m", p=128)
        sc_mla = 192.0 ** -0.5
        sc = 128.0 ** -0.5
        cq_rhs = lambda kc, t: (self.cqn[:, kc, t * TT:(t + 1) * TT], ("cqn", kc, t))
        ckv_rhs = lambda kc, kt: (self.ckv[:, kc, kt * TT:(kt + 1) * TT], ("ckv", kc, kt))
        for pr in range(4):
            sq_, wqp = self.wload(wqbv[:, :, pr * 384:(pr + 1) * 384], [128, 4, 384])
            sk_, wkp = self.wload(wkvbv[:, :, pr * 512:(pr + 1) * 512], [128, 2, 512])
            for i in range(2):
                for (t, b) in self.projA(sq_, wqp, i * 192, 128, cq_rhs, 4, range(NTT)):
                    self.copy(self.q2[:, i, t * TT:(t + 1) * TT], self.ps[b][:], [("ps", b)], [("q2", i, t)])
                for (t, b) in self.projA(sq_, wqp, i * 192 + 128, 64, cq_rhs, 4, range(NTT)):
                    tsl = slice(t * TT, (t + 1) * TT)
                    if gi == 0:
                        self.copy(self.qr2[0:64, i, tsl], self.ps[b][0:64, :], [("ps", b)], [("qr2", i, t)])
                    else:
                        fr, fa = self.ft()
                        self.copy(fa[0:64, :], self.ps[b][0:64, :], [("ps", b)], [fr], eng="dve")
                        self.rope(fa[0:64, :], fr, 64, t, self.qr2[0:64, i, tsl], [("qr2", i, t)])
                for (kt, b) in self.projA(sk_, wkp, i * 256, 128, ckv_rhs, 2, range(nkt)):
                    self.copy(self.k2[:, i, kt * TT:(kt + 1) * TT], self.ps[b][:], [("ps", b)], [("k2", i, kt)], eng="dve")
            for c in range(nkt * 4):
                b = self.bank()
                for i in range(2):
                    for kc in range(2):
                        self.mm(b, self.ps[b][:, i * 128:(i + 1) * 128], self.ckv[:, kc, c * 128:(c + 1) * 128],
                                wkp[:, kc, i * 256 + 128:i * 256 + 256], kc == 0, kc == 1, [("ws", sk_), ("ckv", kc, c // 4)])
                self.copy(self.v2[:, c, :], self.ps[b][:, 0:256], [("ps", b)], [("v2", c)])
            if self.cfg.get("mla_stop", 9) <= 1:
                continue
            self.attention(gi, pr % 2, [(0, 0, 0, None), (1, 1, 1, None)], sc_mla, True)
            if pr % 2 == 1:
                self.wout_group(l, (pr - 1) * 256)
        if self.cfg.get("mix_stop", 9) <= 2:
            return
        if gi == 1:
            self.join([("cqn", oc, t) for oc in range(4) for t in range(NTT)], ["ez"])
            self.build_ez(l)
        for pr in range(2):
            sq_, wqp = self.wload(winv[:, :, S3 + pr * 256:S3 + (pr + 1) * 256], [128, KC, 256])
            sk_, wkp = self.wload(winv[:, :, S3 + 512 + pr * 256:S3 + 512 + (pr + 1) * 256], [128, KC, 256])
            sv_, wvp = self.wload(winv[:, :, S3 + 1024 + pr * 256:S3 + 1024 + (pr + 1) * 256], [128, KC, 256])
            if gi == 1:
                for i in range(2):
                    self.dma("pool", self.k2[:, i, 0:PAST], self.c_nakT[l, 2 * pr + i], writes=[("k2", i, 0)])
                self.dma("pool", self.v2[:, 0:4, :], self.c_nav[l].rearrange("(c p) f -> p c f", p=128)[:, :, pr * 256:(pr + 1) * 256],
                         writes=[("v2", c) for c in range(4)])
            for i in range(2):
                for (t, b) in self.projA(sq_, wqp, i * 128, 128, self.hrhs, KC, range(NTT)):
                    self.copy(self.q2[:, i, t * TT:(t + 1) * TT], self.ps[b][:], [("ps", b)], [("q2", i, t)])
                for (t, b) in self.projA(sk_, wkp, i * 128, 128, self.hrhs, KC, range(NTT)):
                    kt = (koff + t * TT) // TT
                    if gi == 0:
                        fr, fa = self.ft()
                        self.copy(fa, self.ps[b][:], [("ps", b)], [fr], eng="dve")
                        self.copy(self.k2[:, i, koff + t * TT:koff + (t + 1) * TT], fa, [fr], [("k2", i, kt)])
                        self.dma("sp", self.o_nakT[l, 2 * pr + i][:, t * TT:(t + 1) * TT], fa, reads=[fr])
                    else:
                        self.copy(self.k2[:, i, koff + t * TT:koff + (t + 1) * TT], self.ps[b][:], [("ps", b)], [("k2", i, kt)])
            if self.cfg.get("na_stop", 9) <= 1:
                continue
            self.vproj(l, gi, sv_, wvp, kc0, self.o_nav, pr * 256)
            if self.cfg.get("na_stop", 9) <= 2:
                continue
            self.attention(gi, pr % 2, [(i, i, i, (2 * pr + i) if gi == 1 else None) for i in range(2)], sc, False)
            if pr % 2 == 1:
                self.wout_group(l, 1024)
        if self.cfg.get("mix_stop", 9) <= 3:
            return
        sk_, wkp = self.wload(winv[:, :, S5:S5 + 256], [128, KC, 256])
        sv_, wvp = self.wload(winv[:, :, S5 + 256:S5 + 512], [128, KC, 256])
        if gi == 1:
            for i in range(2):
                self.dma("pool", self.k2[:, i, 0:PAST], self.c_gkT[l, i], writes=[("k2", i, 0)])
            self.dma("pool", self.v2[:, 0:4, :], self.c_gv[l].rearrange("(c p) f -> p c f", p=128), writes=[("v2", c) for c in range(4)])

        def normed(b, gcol, t, out_bf, out_res, out_dram):
            r2r, r2 = self.ft()
            self.rstd_bcast([(self.ps[b][:], [("ps", b)])], 128, r2, r2r)
            fr, fa = self.ft()
            self.stt(fa, self.ps[b][:], self.vec[:, gcol:gcol + 1], r2, ALU.mult, ALU.mult, [("ps", b), r2r, "vec"], [fr])
            if gi == 0:
                self.copy(out_bf, fa, [fr], out_res)
                if out_dram is not None:
                    self.dma("sp", out_dram, fa, reads=[fr])
            else:
                self.rope(fa, fr, 128, t, out_bf, out_res)

        for i in range(2):
            for (t, b) in self.projA(sk_, wkp, i * 128, 128, self.hrhs, KC, range(NTT)):
                kt = (koff + t * TT) // TT
                normed(b, vb + VB_GKN, t, self.k2[:, i, koff + t * TT:koff + (t + 1) * TT], [("k2", i, kt)],
                       self.o_gkT[l, i][:, t * TT:(t + 1) * TT] if gi == 0 else None)
        self.vproj(l, gi, sv_, wvp, kc0, self.o_gv, 0)
        for pr in range(2):
            sq_, wqp = self.wload(winv[:, :, S4 + pr * 256:S4 + (pr + 1) * 256], [128, KC, 256])
            for i in range(2):
                for (t, b) in self.projA(sq_, wqp, i * 128, 128, self.hrhs, KC, range(NTT)):
                    normed(b, vb + VB_GQN, t, self.q2[:, i, t * TT:(t + 1) * TT], [("q2", i, t)], None)
            self.attention(gi, pr % 2, [(i, pr, pr, None) for i in range(2)], sc, False)
            if pr % 2 == 1:
                self.wout_group(l, 1536, last=True)


def _fm(a):
    t, d = a.shape
    return np.ascontiguousarray(a.T.reshape(d // 128, 128, t).transpose(1, 0, 2))


def _fm_inv(a):
    p, kc, t = a.shape
    return np.ascontiguousarray(a.transpose(1, 0, 2).reshape(kc * p, t).T)


def _vcols(v):
    return np.asarray(v, np.float32).reshape(-1, 128).T


def _rope_consts():
    def tables(d):
        q = d // 4
        t = np.arange(NT)
        pos = np.stack([t // 64, t % 64], axis=0).astype(np.float32)
        inv = (np.float32(10000.0) ** (-np.arange(q, dtype=np.float32) / np.float32(q))).astype(np.float32)
        cos = np.zeros((d, NT), np.float32)
        sin = np.zeros((d, NT), np.float32)
        rt = np.zeros((d, d), np.float32)
        for p in range(d):
            blk, which, j = p // (2 * q), (p // q) % 2, p % q
            ang = (pos[blk] * inv[j]).astype(np.float32)
            cos[p] = np.cos(ang)
            sin[p] = np.sin(ang)
            if which == 0:
                rt[p + q, p] = -1.0
            else:
                rt[p - q, p] = 1.0
        return cos, sin, rt
    c128, s128, r128 = tables(128)
    c64, s64, r64 = tables(64)
    rot = np.zeros((128, 192), np.float32)
    rot[:, 0:128] = r128
    rot[0:64, 128:192] = r64
    w1 = np.zeros((31, 127), np.float32)
    for dc in range(31):
        w1[dc, dc + 48] = 1.0
    vc = np.zeros((64, 64), np.float32)
    for qc in range(64):
        cs = min(max(qc - 8, 0), 48)
        vc[cs:cs + 16, qc] = 1.0
    return {"cos128": c128, "sin128": s128, "cos64": c64, "sin64": s64, "rotT": rot, "w1": w1, "validc": vc}


def prepare_inputs(inp, cfg, ncores=NCORES):
    f = lambda k: np.asarray(inp[k], dtype=np.float32)
    vec_l = []
    for l in range(L):
        vec_l += [_vcols(f("ada_b")[l]), _vcols(f("norm_g")[l].reshape(-1)), _vcols(f("mla_q_norm")[l]),
                  _vcols(f("mla_kv_norm")[l]), _vcols(f("gqa_q_norm")[l]), _vcols(f("gqa_k_norm")[l])]
    vec_l.append(_vcols(f("final_norm")))
    vecs = np.ascontiguousarray(np.concatenate(vec_l, axis=1))
    shared = {"vecs": vecs}
    for k in ("ada_w", "ffn_wg", "ffn_wu", "ffn_wd", "w_in", "mla_wqb", "mla_wkvb", "w_out"):
        shared[k] = f(k)
    shared.update(_rope_consts())
    rpb = f("na_rpb")
    shared["rpbT"] = np.ascontiguousarray(rpb[:, :, ::-1, :].transpose(3, 0, 1, 2).reshape(31, L, 60))
    xp, xs, c, cctx = f("x_prompt"), f("x_sample"), f("c"), f("c_ctx")
    ckv, krp = f("cache_mla_ckv"), f("cache_mla_krope")
    nak, nav, gk, gv = f("cache_na_k"), f("cache_na_v"), f("cache_gqa_k"), f("cache_gqa_v")
    in_maps = []
    for k in range(ncores):
        m = dict(shared)
        m["xT_p"] = _fm(xp[4 * k:4 * k + 4].reshape(NT, D))
        m["xT_s"] = _fm(xs[k])
        m["condT"] = np.ascontiguousarray(np.stack([_vcols(cctx), _vcols(c[k])], axis=-1))
        m["c_ckvT"] = np.ascontiguousarray(ckv[k].transpose(0, 2, 1).reshape(L, 2, 128, PAST).transpose(0, 2, 1, 3))
        m["c_krT"] = np.ascontiguousarray(krp[k].transpose(0, 2, 1))
        m["c_nakT"] = np.ascontiguousarray(nak[k].transpose(0, 2, 3, 1))
        m["c_nav"] = np.ascontiguousarray(nav[k].reshape(L, PAST, 512))
        m["c_gkT"] = np.ascontiguousarray(gk[k].transpose(0, 2, 3, 1))
        m["c_gv"] = np.ascontiguousarray(gv[k].reshape(L, PAST, 256))
        in_maps.append(m)
    return in_maps


def assemble(outs):
    n = len(outs)
    y_prompt = np.stack([_fm_inv(o["yT_p"]) for o in outs]).reshape(4 * n, 256, D)
    y_sample = np.stack([_fm_inv(o["yT_s"]) for o in outs])
    def per_seq(a):
        nn, l, t, fdim = a.shape
        return np.ascontiguousarray(a.reshape(nn, l, 4, 256, fdim).transpose(0, 2, 1, 3, 4).reshape(nn * 4, l, 256, fdim))
    ckv = np.stack([o["o_ckvT"].transpose(0, 2, 1, 3).reshape(L, 256, NT).transpose(0, 2, 1) for o in outs])
    kr = np.stack([o["o_krT"].transpose(0, 2, 1) for o in outs])
    nak = np.stack([o["o_nakT"].transpose(0, 3, 1, 2).reshape(L, NT, 512) for o in outs])
    nav = np.stack([o["o_nav"] for o in outs])
    gk = np.stack([o["o_gkT"].transpose(0, 3, 1, 2).reshape(L, NT, 256) for o in outs])
    gv = np.stack([o["o_gv"] for o in outs])
    return (y_prompt, y_sample, per_seq(ckv), per_seq(kr),
            per_seq(nak).reshape(4 * n, L, 256, 4, 128), per_seq(nav).reshape(4 * n, L, 256, 4, 128),
            per_seq(gk).reshape(4 * n, L, 256, 2, 128), per_seq(gv).reshape(4 * n, L, 256, 2, 128))


_CACHE = {}


def run(inputs, cfg, ncores=NCORES, trace=False):
    key = tuple(sorted(cfg.items()))
    if key not in _CACHE:
        _CACHE[key] = Builder(dict(cfg)).build()
    nc = _CACHE[key]
    in_maps = prepare_inputs(inputs, cfg, ncores)
    res = run_bass_kernel_spmd(nc, in_maps, core_ids=list(range(ncores)), trace=trace)
    if trace:
        print("exec_time_ns", res.exec_time_ns)
    return res.results


def kernel(**inputs):
    cfg = {"layers": 2, "ffn1": True, "mixer": True, "ffn2": True}
    outs = run(inputs, cfg)
    return tuple(np.ascontiguousarray(a, dtype=np.float32) for a in assemble(outs))
```
